# Optimizing a Trainium2 kernel written in Bass

```python
import math
import jax, jax.numpy as jnp
from jax import lax
import numpy as np

D_MODEL = 2048
BATCH = 16
SEQ = 2048
DEPTH = 4

A_HEADS = 6
A_HEAD_DIM = 64
A_V_DIM = 2 * A_HEAD_DIM
A_WIDTH = A_HEADS * A_V_DIM
Q_BLOCK = 128
B_GROUPS = ((128, 1), (512, 4), (2048, 16))
B_HEADS_PER_GROUP = 4
B_HEAD_DIM = 64
B_HEADS = B_HEADS_PER_GROUP * len(B_GROUPS)
B_WIDTH = B_HEADS * B_HEAD_DIM
BAND_BLOCK = 64
C_WINDOWS = (2, 4, 8, 16)
C_GROUP_DIM = 128
C_WIDTH = C_GROUP_DIM * len(C_WINDOWS)
N_BRANCHES = 3
D_FF = 4 * D_MODEL
REL_BUCKETS = 32
REL_MAX_DISTANCE = 1024
REL_HEADS = A_HEADS + B_HEADS
RMS_EPS = 1e-6
NEG_INF = -1e30

IN_SIZES = (
    A_HEADS * 2 * A_HEAD_DIM,
    A_HEADS * 2 * A_HEAD_DIM,
    A_WIDTH,
    B_WIDTH,
    B_WIDTH,
    B_WIDTH,
    C_WIDTH,
    N_BRANCHES * D_MODEL,
)
IN_COLS = sum(IN_SIZES)
IN_SPLITS = np.cumsum(IN_SIZES)[:-1].tolist()

kernel_name = "hybrid_gated_diffattn_dilated_pool_encoder"


def rms_norm(x, g):
    xf = x.astype(jnp.float32)
    y = xf * lax.rsqrt(jnp.mean(xf * xf, axis=-1, keepdims=True) + RMS_EPS)
    return (y * g.astype(jnp.float32)).astype(x.dtype)


def t5_bucket(rel):
    half = REL_BUCKETS // 2
    exact = half // 2
    n = jnp.abs(rel)
    sign_off = jnp.where(rel > 0, half, 0)
    nf = jnp.maximum(n, 1).astype(jnp.float32)
    large = exact + (jnp.log(nf / exact) / math.log(REL_MAX_DISTANCE / exact)
                     * (half - exact)).astype(jnp.int32)
    large = jnp.minimum(large, half - 1)
    return sign_off + jnp.where(n < exact, n, large)


def diff_attention(q, k, v, table_a, lam, lam_init, subln_g):
    B, S = q.shape[:2]
    nblk = S // Q_BLOCK
    q_blocks = q.reshape(B, nblk, Q_BLOCK, A_HEADS, 2, A_HEAD_DIM).transpose(1, 0, 2, 3, 4, 5)
    starts = jnp.arange(nblk, dtype=jnp.int32) * Q_BLOCK
    kpos = jnp.arange(S, dtype=jnp.int32)
    scale = A_HEAD_DIM ** -0.5

    def one_block(args):
        qb, start = args
        qpos = start + jnp.arange(Q_BLOCK, dtype=jnp.int32)
        rel = kpos[None, :] - qpos[:, None]
        bias = table_a[t5_bucket(rel)].astype(jnp.float32).transpose(2, 0, 1)
        logits = jnp.einsum('bqhcd,bkhcd->bhcqk', qb, k).astype(jnp.float32) * scale
        p = jax.nn.softmax(logits + bias[None, :, None], axis=-1)
        attn = p[:, :, 0] - lam * p[:, :, 1]
        return jnp.einsum('bhqk,bkhe->bqhe', attn.astype(v.dtype), v)

    out = lax.map(one_block, (q_blocks, starts))
    out = out.transpose(1, 0, 2, 3, 4).reshape(B, S, A_HEADS, A_V_DIM)
    out = rms_norm(out, subln_g) * (1.0 - lam_init)
    return out.reshape(B, S, A_WIDTH)


def dilated_group(q, k, v, table_g, window, dilation):
    B, S, Hg, d = q.shape
    r = dilation
    n = window // (2 * r)
    L = S // r
    bs = math.gcd(L, BAND_BLOCK)
    nb = L // bs
    span = bs + 2 * n
    qs = q.reshape(B, nb, bs, r, Hg, d)
    pad = ((0, 0), (n, n), (0, 0), (0, 0), (0, 0))
    kp = jnp.pad(k.reshape(B, L, r, Hg, d), pad)
    vp = jnp.pad(v.reshape(B, L, r, Hg, d), pad)
    blk_start = jnp.arange(nb, dtype=jnp.int32) * bs
    t_off = jnp.arange(span, dtype=jnp.int32)
    s_off = jnp.arange(bs, dtype=jnp.int32)
    idx = blk_start[:, None] + t_off[None, :]
    kb = kp[:, idx]
    vb = vp[:, idx]
    rel = t_off[None, :] - n - s_off[:, None]
    kj = blk_start[:, None, None] + t_off[None, None, :] - n
    valid = (jnp.abs(rel) <= n)[None] & (kj >= 0) & (kj < L)
    bias = table_g[t5_bucket(rel * r)].astype(jnp.float32).transpose(2, 0, 1)
    logits = jnp.einsum('bnqthd,bnkthd->bnthqk', qs, kb).astype(jnp.float32) * (d ** -0.5)
    logits = jnp.where(valid[None, :, None, None], logits + bias[None, None, None], NEG_INF)
    m = jnp.max(logits, axis=-1, keepdims=True)
    e = jnp.exp(logits - m)
    ssum = jnp.sum(e, axis=-1, keepdims=True)
    out = jnp.einsum('bnthqk,bnkthd->bnqthd', (e / ssum).astype(v.dtype), vb)
    lse = (m + jnp.log(ssum))[..., 0]
    out = out.reshape(B, S, Hg, d)
    lse = lse.transpose(0, 1, 4, 2, 3).reshape(B, S, Hg)
    return out, lse


def dilated_mixture(q, k, v, table_b):
    B, S = q.shape[:2]
    outs, lses = [], []
    for g, (window, dilation) in enumerate(B_GROUPS):
        hs = slice(g * B_HEADS_PER_GROUP, (g + 1) * B_HEADS_PER_GROUP)
        o, s = dilated_group(q[:, :, hs], k[:, :, hs], v[:, :, hs], table_b[:, hs], window, dilation)
        outs.append(o)
        lses.append(s)
    outs = jnp.stack(outs, axis=0)
    alpha = jax.nn.softmax(jnp.stack(lses, axis=0), axis=0)
    mixed = (alpha[..., None] * outs).astype(q.dtype)
    return mixed.transpose(1, 2, 0, 3, 4).reshape(B, S, B_WIDTH)


def pool_mixer(c, pool_w, pool_scale):
    B, S, _ = c.shape
    cf = c.astype(jnp.float32)
    cs = jnp.concatenate([jnp.zeros((B, 1, C_WIDTH), jnp.float32), jnp.cumsum(cf, axis=1)], axis=1)
    pos = jnp.arange(S, dtype=jnp.int32)
    diffs = []
    for g, win in enumerate(C_WINDOWS):
        rad = win // 2
        lo = jnp.clip(pos - rad, 0, S)
        hi = jnp.clip(pos + rad + 1, 0, S)
        sl = slice(g * C_GROUP_DIM, (g + 1) * C_GROUP_DIM)
        seg = cs[:, hi, sl] - cs[:, lo, sl]
        cnt = (hi - lo).astype(jnp.float32)[None, :, None]
        diffs.append(seg / cnt - cf[:, :, sl])
    dpool = jnp.stack(diffs, axis=2).astype(c.dtype)
    y = jnp.einsum('bsgc,gce->bsge', dpool, pool_w).reshape(B, S, C_WIDTH)
    return y * pool_scale


def setup_inputs(seed: int = 0) -> dict:
    key = jax.random.key(seed)
    ks = jax.random.split(key, 20)
    f32 = jnp.float32

    def nrm(k, shape, scale):
        return jax.random.normal(k, shape, f32) * scale

    return {
        "x": nrm(ks[0], (BATCH, SEQ, D_MODEL), 1.0),
        "rel_bias_table": nrm(ks[1], (REL_BUCKETS, REL_HEADS), 0.3),
        "norm1_g": 1.0 + nrm(ks[2], (DEPTH, D_MODEL), 0.02),
        "w_in": nrm(ks[3], (DEPTH, D_MODEL, IN_COLS), D_MODEL ** -0.5),
        "lambda_q1": nrm(ks[4], (DEPTH, A_HEAD_DIM), 0.1),
        "lambda_k1": nrm(ks[5], (DEPTH, A_HEAD_DIM), 0.1),
        "lambda_q2": nrm(ks[6], (DEPTH, A_HEAD_DIM), 0.1),
        "lambda_k2": nrm(ks[7], (DEPTH, A_HEAD_DIM), 0.1),
        "subln_g": 1.0 + nrm(ks[8], (DEPTH, A_V_DIM), 0.02),
        "pool_w": nrm(ks[9], (DEPTH, len(C_WINDOWS), C_GROUP_DIM, C_GROUP_DIM), C_GROUP_DIM ** -0.5),
        "pool_scale": 1.0 + nrm(ks[10], (DEPTH, C_WIDTH), 0.1),
        "w_proj_a": nrm(ks[11], (DEPTH, A_WIDTH, D_MODEL), A_WIDTH ** -0.5),
        "w_proj_b": nrm(ks[12], (DEPTH, B_WIDTH, D_MODEL), B_WIDTH ** -0.5),
        "w_proj_c": nrm(ks[13], (DEPTH, C_WIDTH, D_MODEL), C_WIDTH ** -0.5),
        "w_out": nrm(ks[14], (DEPTH, D_MODEL, D_MODEL), D_MODEL ** -0.5),
        "norm2_g": 1.0 + nrm(ks[15], (DEPTH, D_MODEL), 0.02),
        "w_up": nrm(ks[16], (DEPTH, D_MODEL, D_FF), D_MODEL ** -0.5),
        "w_down": nrm(ks[17], (DEPTH, D_FF, D_MODEL), D_FF ** -0.5),
        "final_g": 1.0 + nrm(ks[18], (D_MODEL,), 0.02),
    }


def reference(x, rel_bias_table, norm1_g, w_in, lambda_q1, lambda_k1, lambda_q2, lambda_k2,
              subln_g, pool_w, pool_scale, w_proj_a, w_proj_b, w_proj_c, w_out,
              norm2_g, w_up, w_down, final_g):
    B, S, D = x.shape
    table_a = rel_bias_table[:, :A_HEADS]
    table_b = rel_bias_table[:, A_HEADS:]
    for l in range(DEPTH):
        h = rms_norm(x, norm1_g[l])
        z = h @ w_in[l]
        qa, ka, va, qb, kb, vb, c, gates = jnp.split(z, IN_SPLITS, axis=-1)
        lam_init = 0.8 - 0.6 * math.exp(-0.3 * l)
        lam = (jnp.exp(jnp.sum(lambda_q1[l] * lambda_k1[l]).astype(jnp.float32))
               - jnp.exp(jnp.sum(lambda_q2[l] * lambda_k2[l]).astype(jnp.float32)) + lam_init)
        ya = diff_attention(qa.reshape(B, S, A_HEADS, 2, A_HEAD_DIM),
                            ka.reshape(B, S, A_HEADS, 2, A_HEAD_DIM),
                            va.reshape(B, S, A_HEADS, A_V_DIM),
                            table_a, lam, lam_init, subln_g[l])
        yb = dilated_mixture(qb.reshape(B, S, B_HEADS, B_HEAD_DIM),
                             kb.reshape(B, S, B_HEADS, B_HEAD_DIM),
                             vb.reshape(B, S, B_HEADS, B_HEAD_DIM), table_b)
        yc = pool_mixer(c, pool_w[l], pool_scale[l])
        g = jax.nn.sigmoid(gates.astype(jnp.float32)).reshape(B, S, N_BRANCHES, D).astype(x.dtype)
        merged = (g[:, :, 0] * (ya @ w_proj_a[l])
                  + g[:, :, 1] * (yb @ w_proj_b[l])
                  + g[:, :, 2] * (yc @ w_proj_c[l]))
        x = x + merged @ w_out[l]
        h2 = rms_norm(x, norm2_g[l])
        u = jax.nn.relu(h2 @ w_up[l])
        x = x + (u * u) @ w_down[l]
    return rms_norm(x, final_g)
```

```python
import math
import os
from contextlib import ExitStack

import numpy as np
import concourse.bass as bass
import concourse.mybir as mybir
from concourse.bass_utils import run_bass_kernel_spmd

F32 = mybir.dt.float32
BF16 = mybir.dt.bfloat16
AF = mybir.ActivationFunctionType
ALU = mybir.AluOpType
AX = mybir.AxisListType

D = 2048
S = 2048
DEPTH = 4
NSEQ = 2
IN_COLS = 11264
DFF = 8192
EPS = 1e-6
U0 = 1920
RW = 3968
GW = 4096
ENG = ("pe", "act", "dve", "pool", "sp")
NDMASEM = 20


class Res:
    __slots__ = ("name", "w", "r", "rd")

    def __init__(self, name=""):
        self.name = name
        self.w = None
        self.r = {}
        self.rd = []


class Op:
    __slots__ = ("eng", "fn", "deps", "need_inc", "idx", "dma", "sem", "semval", "semprev")

    def __init__(self, eng, fn, dma):
        self.eng = eng
        self.fn = fn
        self.dma = dma
        self.deps = ()
        self.need_inc = False
        self.idx = 0
        self.sem = None
        self.semval = 0
        self.semprev = 0


class Sched:
    def __init__(self):
        self.ops = {e: [] for e in ENG}
        self.since_barrier = []
        self.last = {e: None for e in ENG}
        self.barrier_deps = {e: [] for e in ENG}

    def op(self, eng, fn, reads=(), writes=(), dma=False):
        o = Op(eng, fn, dma)
        deps = set(self.barrier_deps[eng])
        self.barrier_deps[eng] = []
        for r in reads:
            if r.w is not None:
                deps.add(r.w)
        for w in writes:
            if w.w is not None:
                deps.add(w.w)
            deps.update(w.r.values())
            deps.update(w.rd)
        dl = []
        for d in deps:
            if d is o:
                continue
            if (not d.dma) and (not dma) and d.eng == "pe" and eng == "pe":
                continue
            dl.append(d)
            if not d.dma:
                d.need_inc = True
        o.deps = dl
        for r in reads:
            if dma:
                r.rd.append(o)
            else:
                r.r[eng] = o
        for w in writes:
            w.w = o
            w.r = {}
            w.rd = []
        self.ops[eng].append(o)
        self.last[eng] = o
        if dma:
            self.since_barrier.append(o)
        return o

    def barrier(self):
        deps = list(self.since_barrier)
        for e in ENG:
            if self.last[e] is not None:
                deps.append(self.last[e])
        self.since_barrier = []
        for e in ENG:
            self.barrier_deps[e] = list(deps)

    def finish(self):
        self.barrier()
        return self.op("sp", None)

    def emit(self, nc, block, engsem, dmasems):
        for e in ENG:
            c = 0
            nd = 0
            for o in self.ops[e]:
                if o.dma:
                    k = nd % NDMASEM
                    o.sem = dmasems[e][k]
                    o.semprev = 16 * (nd // NDMASEM)
                    o.semval = o.semprev + 16
                    nd += 1
                elif o.need_inc:
                    c += 1
                    o.idx = c
        handles = {"pe": block.tensor, "act": block.scalar, "dve": block.vector,
                   "pool": block.gpsimd, "sp": block.sync}

        def make(e):
            def run(eng):
                seen = {}
                for o in self.ops[e]:
                    waits = {}
                    for d in o.deps:
                        if d.dma:
                            key, val = d.sem, d.semval
                        else:
                            key, val = engsem[d.eng], d.idx
                        kid = id(key)
                        if kid not in waits or waits[kid][1] < val:
                            waits[kid] = (key, val)
                    if o.dma and o.semprev > 0:
                        kid = id(o.sem)
                        if kid not in waits or waits[kid][1] < o.semprev:
                            waits[kid] = (o.sem, o.semprev)
                    for kid, (key, val) in waits.items():
                        if seen.get(kid, 0) >= val:
                            continue
                        eng.wait_ge(key, val)
                        seen[kid] = val
                    if o.fn is None:
                        continue
                    ins = o.fn(eng)
                    if o.dma:
                        ins.then_inc(o.sem, 16)
                    elif o.need_inc:
                        ins.then_inc(engsem[e], 1)
            return run

        for e in ENG:
            handles[e](make(e))


class Arena:
    def __init__(self, nc, base, limit):
        self.nc = nc
        self.base = base
        self.cur = base
        self.limit = limit
        self.n = 0

    def reset(self):
        self.cur = self.base

    def alloc(self, shape, dtype, name="t"):
        nbytes = int(np.prod(shape[1:])) * (4 if dtype == F32 else 2)
        nbytes = (nbytes + 63) // 64 * 64
        off = self.cur
        self.cur += nbytes
        assert self.cur <= self.limit, f"arena overflow {self.cur} > {self.limit} ({name})"
        self.n += 1
        return self.nc.alloc_sbuf_tensor_at(f"{name}_{self.n}", list(shape), dtype, offset=off)


def rot(lst, i):
    return lst[i % len(lst)]


class Prog:
    def __init__(self, n_layers=DEPTH, n_seq=NSEQ, taps=False, stop_after=None):
        self.n_layers = n_layers
        self.n_seq = n_seq
        self.taps = taps
        self.stop_after = stop_after
        nc = bass.Bass("TRN2", target_bir_lowering=False)
        self.nc = nc
        self.sc = Sched()
        T = NSEQ * S

        def din(name, shape):
            return nc.dram_tensor(name, list(shape), F32, kind="ExternalInput").ap()

        self.x = din("x", [T, D])
        self.table = din("rel_bias_table", [32, 18])
        self.norm1_g = din("norm1_g", [DEPTH, D])
        self.w_in = din("w_in", [DEPTH, D, IN_COLS])
        self.lq1 = din("lambda_q1", [DEPTH, 64])
        self.lk1 = din("lambda_k1", [DEPTH, 64])
        self.lq2 = din("lambda_q2", [DEPTH, 64])
        self.lk2 = din("lambda_k2", [DEPTH, 64])
        self.subln_g = din("subln_g", [DEPTH, 128])
        self.pool_w = din("pool_w", [DEPTH, 4, 128, 128])
        self.pool_scale = din("pool_scale", [DEPTH, 512])
        self.w_proj_a = din("w_proj_a", [DEPTH, 768, D])
        self.w_proj_b = din("w_proj_b", [DEPTH, 768, D])
        self.w_proj_c = din("w_proj_c", [DEPTH, 512, D])
        self.w_out = din("w_out", [DEPTH, D, D])
        self.norm2_g = din("norm2_g", [DEPTH, D])
        self.w_up = din("w_up", [DEPTH, D, DFF])
        self.w_down = din("w_down", [DEPTH, DFF, D])
        self.final_g = din("final_g", [1, D])
        self.c_onehot = din("c_onehot", [32, GW])
        self.c_mask = din("c_mask", [18, GW])
        self.c_ident = din("c_ident", [128, 128])
        self.c_antiid = din("c_antiid", [128, 128])
        self.c_invcnt = din("c_invcnt", [4, S])

        self.out = nc.dram_tensor("out", [T, D], F32, kind="ExternalOutput").ap()

        def scr(name, shape, dt):
            kind = "ExternalOutput" if taps else "Internal"
            return nc.dram_tensor(name, list(shape), dt, kind=kind).ap()

        self.grow = scr("s_grow", [18, GW], BF16)
        self.rscr = scr("s_r", [18, 128, RW], BF16)
        self.qk = scr("s_qk", [24 * 128, S], BF16)
        self.va = scr("s_va", [S, 768], BF16)
        self.vb = scr("s_vb", [S, 768], BF16)
        self.cscr = scr("s_c", [512, S], F32)
        self.gscr = scr("s_g", [6144, S], BF16)
        self.yT = scr("s_yT", [16 * 128, S], BF16)
        self.mT = scr("s_mT", [16 * 128, S], BF16)
        self.xa = scr("s_xa", [T, D], F32)
        self.xb = scr("s_xb", [T, D], F32)
        self.us = scr("s_u", [16, 128, 64, 128], BF16)

        self.ar = Arena(nc, 20480, 229344)
        self.ident = self.ar.alloc([128, 128], BF16, "ident")
        self.antiid = self.ar.alloc([128, 128], BF16, "antiid")
        self.ar.base = self.ar.cur
        self.banks = [nc.alloc_psum_tensor(f"bank{i}", [128, 512], F32) for i in range(8)]

    def dma(self, eng, out, in_, reads=(), writes=()):
        return self.sc.op(eng, lambda e: e.dma_start(out=out, in_=in_), reads, writes, dma=True)

    def phase_begin(self):
        self.sc.barrier()
        self.ar.reset()

    def setup(self):
        sc, ar = self.sc, self.ar
        self.phase_begin()
        r_const = Res("const")
        identf = ar.alloc([128, 128], F32, "identf")
        tab = ar.alloc([32, 18], F32, "tab")
        oh = ar.alloc([32, GW], F32, "oh")
        msk = ar.alloc([18, GW], F32, "msk")
        gf = ar.alloc([18, GW], F32, "gf")
        gb = ar.alloc([18, GW], BF16, "gb")
        r_in = Res()
        self.dma("pool", self.ident[:], self.c_ident, writes=[r_const])
        self.dma("pool", self.antiid[:], self.c_antiid, writes=[r_const])
        self.dma("sp", tab[:], self.table, writes=[r_in])
        self.dma("sp", oh[:], self.c_onehot, writes=[r_in])
        self.dma("sp", msk[:], self.c_mask, writes=[r_in])
        r_gf = Res()
        for i in range(GW // 512):
            b = self.banks[i % 4]
            rb = Res()
            sc.op("pe", lambda e, b=b, i=i: e.matmul(b[0:18, :], tab[:, :], oh[:, i * 512:(i + 1) * 512],
                                                     start=True, stop=True), [r_in], [rb])
            sc.op("act", lambda e, b=b, i=i: e.activation(gf[:, i * 512:(i + 1) * 512], b[0:18, :], AF.Exp),
                  [rb], [r_gf])
        r_gb = Res()
        sc.op("dve", lambda e: e.tensor_tensor(gb[:, :], gf[:, :], msk[:, :], ALU.mult), [r_gf, r_in], [r_gb])
        r_grow = Res()
        self.dma("sp", self.grow, gb[:, :], [r_gb], [r_grow])
        t1 = [ar.alloc([128, RW], BF16, "t1") for _ in range(2)]
        rt = [ar.alloc([128, RW], BF16, "rt") for _ in range(2)]
        r_t1 = [Res(), Res()]
        r_rt = [Res(), Res()]
        r_bank = [Res() for _ in range(8)]
        nb = 0
        for h in range(18):
            src = bass.AP(self.grow.tensor, h * GW, [[1, 128], [1, RW]])
            self.dma("sp", t1[h % 2][:, :], src, [r_grow], [r_t1[h % 2]])
            for i in range((RW + 511) // 512):
                w = min(512, RW - i * 512)
                bi = nb % 8
                nb += 1
                b = self.banks[bi]
                sc.op("pe", lambda e, b=b, i=i, w=w, h=h: e.matmul(
                    b[:, 0:w], self.antiid[:, :], t1[h % 2][:, i * 512:i * 512 + w], start=True, stop=True),
                    [r_t1[h % 2], r_const], [r_bank[bi]])
                eng = "act" if i % 2 == 0 else "dve"
                if eng == "act":
                    sc.op("act", lambda e, b=b, i=i, w=w, h=h: e.copy(rt[h % 2][:, i * 512:i * 512 + w], b[:, 0:w]),
                          [r_bank[bi]], [r_rt[h % 2]])
                else:
                    sc.op("dve", lambda e, b=b, i=i, w=w, h=h: e.tensor_copy(rt[h % 2][:, i * 512:i * 512 + w], b[:, 0:w]),
                          [r_bank[bi]], [r_rt[h % 2]])
            self.dma("sp", self.rscr[h], rt[h % 2][:, :], [r_rt[h % 2]], [])

    def norm_to_hT(self, xsrc, grow_ap, hT, r_hT):
        sc, ar = self.sc, self.ar
        gbc = ar.alloc([128, D], F32, "gbc")
        r_g = Res()
        self.dma("sp", gbc[:, :], grow_ap.partition_broadcast(128), writes=[r_g])
        xt = [ar.alloc([128, D], F32, "xt") for _ in range(2)]
        r_xt = [Res(), Res()]
        junk = ar.alloc([128, D], BF16, "junk")
        r_junk = Res()
        ss = [ar.alloc([128, 1], F32, "ss") for _ in range(2)]
        r_ss = [Res(), Res()]
        xn = [ar.alloc([128, D], BF16, "xn") for _ in range(2)]
        r_xn = [Res(), Res()]
        r_b = [Res(), Res()]
        for tt in range(S // 128):
            k = tt % 2
            self.dma("sp", xt[k][:, :], xsrc[tt * 128:(tt + 1) * 128, :], writes=[r_xt[k]])
            sc.op("act", lambda e, k=k: e.activation(junk[:, :], xt[k][:, :], AF.Square, scale=1.0 / math.sqrt(D),
                                                     accum_out=ss[k][:, :]),
                  [r_xt[k]], [r_junk, r_ss[k]])
            sc.op("act", lambda e, k=k: e.activation(ss[k][:, :], ss[k][:, :], AF.Sqrt, bias=EPS),
                  [r_ss[k]], [r_ss[k]])
            sc.op("dve", lambda e, k=k: e.reciprocal(ss[k][:, :], ss[k][:, :]),
                  [r_ss[k]], [r_ss[k]])
            sc.op("dve", lambda e, k=k: e.scalar_tensor_tensor(xn[k][:, :], xt[k][:, :], ss[k][:, 0:1], gbc[:, :],
                                                               ALU.mult, ALU.mult),
                  [r_xt[k], r_ss[k], r_g], [r_xn[k]])
            for hf in range(2):
                pb = self.banks[hf].ap().bitcast(BF16)

                def tr(e, k=k, hf=hf, pb=pb):
                    ins = None
                    for j in range(8):
                        kc = hf * 8 + j
                        ins = e.transpose(pb[:, j * 128:(j + 1) * 128], xn[k][:, kc * 128:(kc + 1) * 128],
                                          self.ident[:, :])
                    return ins
                sc.op("pe", tr, [r_xn[k]], [r_b[hf]])
                dst = hT[:, hf * 8:(hf + 1) * 8, tt * 128:(tt + 1) * 128]
                srcp = pb.rearrange("p (j q) -> p j q", j=8)
                if hf == 0:
                    sc.op("act", lambda e, dst=dst, srcp=srcp: e.copy(dst, srcp), [r_b[hf]], [r_hT])
                else:
                    sc.op("dve", lambda e, dst=dst, srcp=srcp: e.tensor_copy(dst, srcp), [r_b[hf]], [r_hT])

    def phase_a(self, l, s, xsrc):
        sc, ar = self.sc, self.ar
        self.phase_begin()
        hT = ar.alloc([128, 16, S], BF16, "hT")
        r_hT = Res()
        self.norm_to_hT(xsrc, self.norm1_g[l:l + 1, :], hT, r_hT)
        wb = [ar.alloc([128, 16, 512], BF16, "wb") for _ in range(2)]
        r_wb = [Res(), Res()]
        stg = [ar.alloc([128, S], BF16, "stg") for _ in range(3)]
        r_stg = [Res() for _ in range(3)]
        stgv = [ar.alloc([128, 512], BF16, "stgv") for _ in range(3)]
        r_stgv = [Res() for _ in range(3)]
        stgc = [ar.alloc([128, S], F32, "stgc") for _ in range(2)]
        r_stgc = [Res() for _ in range(2)]
        bks = self.banks[2:8]
        r_bk = [Res() for _ in range(6)]
        nbk = 0
        nstg = 0
        nstgv = 0
        nstgc = 0
        nev = 0
        wv = self.w_in[l].rearrange("(kc p) n -> p kc n", p=128)
        for j in range(IN_COLS // 512):
            w = wb[j % 2]
            rw = r_wb[j % 2]
            self.dma("pool", w[:, :, :], wv[:, :, j * 512:(j + 1) * 512], writes=[rw])
            lc = 0
            while lc < 4:
                cc = 4 * j + lc
                zform = (12 <= cc < 18) or (30 <= cc < 36)
                if zform:
                    n = 1
                    while lc + n < 4 and ((12 <= cc + n < 18) or (30 <= cc + n < 36)):
                        n += 1
                    ncol = n * 128
                    if cc < 18:
                        vdst, vc0 = self.va, (cc - 12) * 128
                    else:
                        vdst, vc0 = self.vb, (cc - 30) * 128
                    for tt in range(16):
                        bi = nbk % 6
                        nbk += 1
                        b = bks[bi]

                        def mm(e, b=b, w=w, tt=tt, lc=lc, ncol=ncol):
                            ins = None
                            for kc in range(16):
                                ins = e.matmul(b[:, 0:ncol], hT[:, kc, tt * 128:(tt + 1) * 128],
                                               w[:, kc, lc * 128:lc * 128 + ncol], start=(kc == 0), stop=(kc == 15))
                            return ins
                        sc.op("pe", mm, [r_hT, rw], [r_bk[bi]])
                        si = nstgv % 3
                        nstgv += 1
                        st = stgv[si]
                        if nev % 2 == 0:
                            sc.op("act", lambda e, st=st, b=b, ncol=ncol: e.copy(st[:, 0:ncol], b[:, 0:ncol]),
                                  [r_bk[bi]], [r_stgv[si]])
                        else:
                            sc.op("dve", lambda e, st=st, b=b, ncol=ncol: e.tensor_copy(st[:, 0:ncol], b[:, 0:ncol]),
                                  [r_bk[bi]], [r_stgv[si]])
                        nev += 1
                        self.dma("sp", vdst[tt * 128:(tt + 1) * 128, vc0:vc0 + ncol], st[:, 0:ncol],
                                 [r_stgv[si]], [])
                    lc += n
                    continue
                if cc < 12:
                    kind, dst = "qk", self.qk[cc * 128:(cc + 1) * 128, :]
                elif cc < 30:
                    kind, dst = "qk", self.qk[(cc - 6) * 128:(cc - 5) * 128, :]
                elif cc < 40:
                    kind, dst = "c", self.cscr[(cc - 36) * 128:(cc - 35) * 128, :]
                else:
                    kind, dst = "g", self.gscr[(cc - 40) * 128:(cc - 39) * 128, :]
                if kind == "c":
                    si = nstgc % 2
                    nstgc += 1
                    st, rs = stgc[si], r_stgc[si]
                else:
                    si = nstg % 3
                    nstg += 1
                    st, rs = stg[si], r_stg[si]
                for tc in range(4):
                    bi = nbk % 6
                    nbk += 1
                    b = bks[bi]

                    def mm(e, b=b, w=w, tc=tc, lc=lc):
                        ins = None
                        for kc in range(16):
                            ins = e.matmul(b[:, :], w[:, kc, lc * 128:(lc + 1) * 128],
                                           hT[:, kc, tc * 512:(tc + 1) * 512], start=(kc == 0), stop=(kc == 15))
                        return ins
                    sc.op("pe", mm, [r_hT, rw], [r_bk[bi]])
                    o = st[:, tc * 512:(tc + 1) * 512]
                    if kind == "g":
                        sc.op("act", lambda e, o=o, b=b: e.activation(o, b[:, :], AF.Sigmoid), [r_bk[bi]], [rs])
                    elif nev % 2 == 0:
                        sc.op("act", lambda e, o=o, b=b: e.copy(o, b[:, :]), [r_bk[bi]], [rs])
                        nev += 1
                    else:
                        sc.op("dve", lambda e, o=o, b=b: e.tensor_copy(o, b[:, :]), [r_bk[bi]], [rs])
                        nev += 1
                self.dma("sp", dst, st[:, :], [rs], [])
                lc += 1


    def attn_run(self, steps, nst=3):
        sc = self.sc
        stb = self.banks[0:nst]
        r_st = [Res() for _ in range(nst)]
        Et = [self.ar.alloc([128, 512], BF16, "E") for _ in range(nst)]
        Pt = [self.ar.alloc([128, 512], BF16, "P") for _ in range(nst)]
        r_E = [Res() for _ in range(nst)]
        r_P = [Res() for _ in range(nst)]

        def emit_st(st, i):
            b = stb[i % nst]
            hs = slice(st["half"] * 64, st["half"] * 64 + 64)
            kb, qc = st["kb"], st["qc"]
            qT, kT, R = st["qT"], st["kT"], st["R"]
            sc.op("pe", lambda e: e.matmul(b[:, :], kT[hs, kb * 128:(kb + 1) * 128], qT[hs, qc * 512:(qc + 1) * 512],
                                           start=True, stop=True), st["r_in"], [r_st[i % nst]])
            E = Et[i % nst]
            sc.op("act", lambda e: e.activation(E[:, :], b[:, :], AF.Exp, scale=0.125), [r_st[i % nst]], [r_E[i % nst]])
            P = Pt[i % nst]
            s0 = U0 - (kb * 128 - qc * 512)
            eng = "dve" if i % 2 == 0 else "pool"
            sc.op(eng, lambda e: e.tensor_tensor(P[:, :], E[:, :], R[:, s0:s0 + 512], ALU.mult),
                  [r_E[i % nst]] + st["r_in"], [r_P[i % nst]])

        def emit_av(st, i):
            P = Pt[i % nst]
            v = st["V"](st["kb"])
            accs = st["acc"]
            first, last = st["first"], st["last"]

            def f(e):
                ins = None
                for j in range(4):
                    ins = e.matmul(accs[j], P[:, j * 128:(j + 1) * 128], v, start=(first and st["jstart"][j]),
                                   stop=last, skip_group_check=True)
                return ins
            sc.op("pe", f, [r_P[i % nst]] + st["r_in"], [st["r_acc"]])
            if last and st["fin"] is not None:
                st["fin"]()
            if st.get("post") is not None:
                st["post"]()

        n = len(steps)
        for i in range(n + 1):
            if i < n:
                emit_st(steps[i], i)
            if i >= 1:
                emit_av(steps[i - 1], i - 1)

    def phase_b(self, l, s):
        sc, ar = self.sc, self.ar
        self.phase_begin()
        lam_init = 0.8 - 0.6 * math.exp(-0.3 * l)
        lv = [ar.alloc([128, 64], F32, "lv") for _ in range(4)]
        r_lv = Res()
        for t, src in zip(lv, (self.lq1, self.lk1, self.lq2, self.lk2)):
            self.dma("sp", t[:, :], src[l:l + 1, :].partition_broadcast(128), writes=[r_lv])
        sm = ar.alloc([128, 8], F32, "sm")
        r_sm = Res()
        pr = ar.alloc([128, 64], F32, "pr")
        r_pr = Res()
        for i in range(2):
            sc.op("dve", lambda e, i=i: e.tensor_tensor(pr[:, :], lv[2 * i][:, :], lv[2 * i + 1][:, :], ALU.mult),
                  [r_lv], [r_pr])
            sc.op("dve", lambda e, i=i: e.reduce_sum(sm[:, i:i + 1], pr[:, :], AX.X), [r_pr], [r_sm])
        sc.op("act", lambda e: e.activation(sm[:, 2:4], sm[:, 0:2], AF.Exp), [r_sm], [r_sm])
        sc.op("dve", lambda e: e.tensor_tensor(sm[:, 4:5], sm[:, 3:4], sm[:, 2:3], ALU.subtract), [r_sm], [r_sm])
        sc.op("dve", lambda e: e.tensor_scalar_add(sm[:, 5:6], sm[:, 4:5], -lam_init), [r_sm], [r_sm])
        neglam = sm[:, 5:6]
        gsub = ar.alloc([128, 1], F32, "gsub")
        r_gs = Res()
        self.dma("sp", gsub[:, :], self.subln_g[l].rearrange("(e o) -> e o", o=1), writes=[r_gs])
        sc.op("dve", lambda e: e.tensor_scalar_mul(gsub[:, :], gsub[:, :], 1.0 - lam_init), [r_gs], [r_gs])
        qT = [ar.alloc([128, S], BF16, "qT") for _ in range(2)]
        kT = [ar.alloc([128, S], BF16, "kT") for _ in range(2)]
        V = [ar.alloc([128, 16, 130], BF16, "V") for _ in range(2)]
        R = [ar.alloc([128, RW], BF16, "R") for _ in range(2)]
        r_hd = [Res(), Res()]
        for k in range(2):
            sc.op("dve", lambda e, k=k: e.memset(V[k][:, :, 128:129], 1.0), [], [r_hd[k]])
        o0 = ar.alloc([128, 16, 128], F32, "o0")
        r_o0 = Res()
        ot = [ar.alloc([128, 128], F32, "ot") for _ in range(2)]
        r_ot = [Res(), Res()]
        jk = ar.alloc([128, 128], F32, "jk")
        r_jk = Res()
        st_ = [ar.alloc([128, 8], F32, "st") for _ in range(4)]
        r_stt = [Res() for _ in range(4)]
        ybf = [ar.alloc([128, 128], BF16, "ybf") for _ in range(4)]
        r_ybf = [Res() for _ in range(4)]
        ystage = [ar.alloc([128, S], BF16, "ystage") for _ in range(2)]
        r_ys = [Res(), Res()]
        tb = self.banks[7].ap().bitcast(BF16)
        r_tb = Res()
        accb = [(self.banks[3], self.banks[4]), (self.banks[5], self.banks[6])]
        r_acc = [Res(), Res()]
        cnt = {"fin": 0}
        steps = []
        gcount = 0
        units = []
        for h in range(6):
            k = h % 2

            def load(h=h, k=k):
                self.dma("sp", qT[k][:, :], self.qk[h * 128:(h + 1) * 128, :], writes=[r_hd[k]])
                self.dma("sp", kT[k][:, :], self.qk[(6 + h) * 128:(7 + h) * 128, :], writes=[r_hd[k]])
                self.dma("sp", V[k][:, :, 0:128],
                         self.va[:, h * 128:(h + 1) * 128].rearrange("(kb p) e -> p kb e", p=128), writes=[r_hd[k]])
                self.dma("sp", R[k][:, :], self.rscr[h], writes=[r_hd[k]])
            usteps = []
            for c in range(2):
                for qc in range(4):
                    par = gcount % 2
                    gcount += 1
                    banks = accb[par]
                    accs = [banks[j // 2][:, (j % 2) * 129:(j % 2) * 129 + 129] for j in range(4)]

                    def fin(h=h, c=c, qc=qc, accs=accs, par=par, k=k):
                        for j in range(4):
                            qt = qc * 4 + j
                            a = accs[j]
                            fi = cnt["fin"]
                            cnt["fin"] += 1
                            sm_ = st_[fi % 4]
                            rs = r_stt[fi % 4]
                            sc.op("dve", lambda e, a=a, sm_=sm_: e.reciprocal(sm_[:, 0:1], a[:, 128:129]),
                                  [r_acc[par]], [rs])
                            if c == 0:
                                sc.op("dve", lambda e, a=a, sm_=sm_, qt=qt: e.tensor_scalar(
                                    o0[:, qt, :], a[:, 0:128], sm_[:, 0:1], None, ALU.mult),
                                    [r_acc[par], rs], [r_o0])
                                continue
                            sc.op("dve", lambda e, sm_=sm_: e.tensor_tensor(sm_[:, 1:2], sm_[:, 0:1], neglam, ALU.mult),
                                  [rs, r_sm], [rs])
                            o = ot[fi % 2]
                            ro = r_ot[fi % 2]
                            sc.op("dve", lambda e, a=a, sm_=sm_, qt=qt, o=o: e.scalar_tensor_tensor(
                                o[:, :], a[:, 0:128], sm_[:, 1:2], o0[:, qt, :], ALU.mult, ALU.add),
                                [r_acc[par], rs, r_o0], [ro])
                            sc.op("dve", lambda e, o=o: e.tensor_tensor(jk[:, :], o[:, :], o[:, :], ALU.mult),
                                  [ro], [r_jk])
                            sc.op("dve", lambda e, sm_=sm_: e.reduce_sum(sm_[:, 2:3], jk[:, :], AX.X), [r_jk], [rs])
                            sc.op("act", lambda e, sm_=sm_: e.activation(sm_[:, 3:4], sm_[:, 2:3], AF.Ln,
                                                                           bias=EPS, scale=1.0 / 128.0), [rs], [rs])
                            sc.op("act", lambda e, sm_=sm_: e.activation(sm_[:, 4:5], sm_[:, 3:4], AF.Exp, scale=-0.5),
                                  [rs], [rs])
                            yb_ = ybf[j]
                            ry = r_ybf[j]
                            sc.op("dve", lambda e, o=o, sm_=sm_, yb_=yb_: e.tensor_scalar(
                                yb_[:, :], o[:, :], sm_[:, 4:5], None, ALU.mult), [ro, rs], [ry])
                        if c == 1:
                            def trs(e):
                                ins = None
                                for j in range(4):
                                    ins = e.transpose(tb[:, j * 128:(j + 1) * 128], ybf[j][:, :], self.ident[:, :])
                                return ins
                            sc.op("pe", trs, list(r_ybf), [r_tb])
                            ys = ystage[h % 2]
                            sc.op("dve", lambda e, ys=ys, qc=qc: e.tensor_scalar(
                                ys[:, qc * 512:(qc + 1) * 512], tb[:, 0:512], gsub[:, 0:1], None, ALU.mult),
                                [r_tb, r_gs], [r_ys[h % 2]])
                        if c == 1 and qc == 3:
                            self.dma("sp", self.yT[h * 128:(h + 1) * 128, :], ystage[h % 2][:, :], [r_ys[h % 2]], [])

                    for kb in range(16):
                        usteps.append(dict(qT=qT[k], kT=kT[k], half=c, R=R[k],
                                           V=(lambda kb, k=k: V[k][:, kb, 0:129]), E=128, qc=qc, kb=kb,
                                           first=(kb == 0), last=(kb == 15), acc=accs, r_in=[r_hd[k]],
                                           jstart=(True, False, True, False),
                                           r_acc=r_acc[par], fin=(fin if kb == 15 else None)))
            units.append((load, usteps))
        self.run_units(units)

    def run_units(self, units):
        steps = []
        for u, (load, usteps) in enumerate(units):
            if u < 2:
                load()
            if u + 2 < len(units):
                usteps[-1]["post"] = units[u + 2][0]
            steps.extend(usteps)
        self.attn_run(steps)

    def phase_c(self, l, s):
        sc, ar = self.sc, self.ar
        self.phase_begin()
        qT = [ar.alloc([128, S], BF16, "qT") for _ in range(2)]
        kT = [ar.alloc([128, S], BF16, "kT") for _ in range(2)]
        V = [ar.alloc([128, 16, 132], BF16, "V") for _ in range(2)]
        R = [ar.alloc([128, RW], BF16, "R") for _ in range(2)]
        r_ck = [Res(), Res()]
        r_R = [Res(), Res()]
        for k in range(2):
            sc.op("dve", lambda e, k=k: e.memset(V[k][:, :, 64:65], 1.0), [], [r_ck[k]])
            sc.op("dve", lambda e, k=k: e.memset(V[k][:, :, 130:131], 1.0), [], [r_ck[k]])
        nd = [[ar.alloc([128, 16, 65], F32, "nd") for _ in range(2)] for _ in range(3)]
        r_nd = [[Res() for _ in range(2)] for _ in range(3)]
        den = [ar.alloc([128, 16], F32, "den") for _ in range(2)]
        r_den = [Res(), Res()]
        ybt = [ar.alloc([128, 16, 128], BF16, "ybt") for _ in range(3)]
        r_ybt = [Res() for _ in range(3)]
        ystage = [ar.alloc([128, S], BF16, "ystage") for _ in range(2)]
        r_ys = [Res(), Res()]
        tb = [self.banks[i].ap().bitcast(BF16) for i in (5, 6, 7)]
        r_tb = [Res() for _ in range(3)]
        accb = [self.banks[3], self.banks[4]]
        r_acc = [Res(), Res()]
        cnt = {"t": 0, "ys": 0}
        steps = []
        gcount = 0
        nload = 0
        nR = 0
        groups = ((128, 1), (512, 4), (2048, 16))
        units = []
        for ip in range(2):
            for g, (win, r) in enumerate(groups):
                k = nload % 2
                nload += 1
                ch = g * 2 + ip
                for half in range(2):
                    hh = g * 4 + 2 * ip + half
                    kr = nR % 2
                    nR += 1

                    def load(k=k, ch=ch, half=half, hh=hh, kr=kr, g=g, ip=ip):
                        if half == 0:
                            self.dma("sp", qT[k][:, :], self.qk[(12 + ch) * 128:(13 + ch) * 128, :], writes=[r_ck[k]])
                            self.dma("sp", kT[k][:, :], self.qk[(18 + ch) * 128:(19 + ch) * 128, :], writes=[r_ck[k]])
                            for hf in range(2):
                                h2 = g * 4 + 2 * ip + hf
                                self.dma("sp", V[k][:, :, hf * 66:hf * 66 + 64],
                                         self.vb[:, h2 * 64:(h2 + 1) * 64].rearrange("(kb p) e -> p kb e", p=128),
                                         writes=[r_ck[k]])
                        self.dma("sp", R[kr][:, :], self.rscr[6 + hh], writes=[r_R[kr]])
                    usteps = []
                    span = (win // (2 * r)) * r
                    for qc in range(4):
                        q0 = qc * 512
                        kb_lo = max(0, (q0 - span) // 128)
                        kb_hi = min(15, (q0 + 511 + span) // 128)
                        kbs = list(range(kb_lo, kb_hi + 1))
                        par = gcount % 2
                        gcount += 1
                        bank = accb[par]
                        accs = [bank[:, j * 65:j * 65 + 65] for j in range(4)]

                        def fin(g=g, half=half, qc=qc, bank=bank, par=par, ip=ip):
                            sc.op("dve", lambda e: e.tensor_copy(
                                nd[g][half][:, qc * 4:(qc + 1) * 4, :],
                                bank[:, 0:260].rearrange("p (j e) -> p j e", j=4)),
                                [r_acc[par]], [r_nd[g][half]])
                            if g == 2 and half == 1 and qc == 3 and not os.environ.get('KDBG_NOCOMB'):
                                self.c_combine(ip, nd, r_nd, den, r_den, ybt, r_ybt, ystage, r_ys, tb, r_tb, cnt)

                        for kb in kbs:
                            usteps.append(dict(qT=qT[k], kT=kT[k], half=half, R=R[kr],
                                               V=(lambda kb, k=k, half=half: V[k][:, kb, half * 66:half * 66 + 65]),
                                               E=64, qc=qc, kb=kb, first=(kb == kbs[0]), last=(kb == kbs[-1]),
                                               acc=accs, r_in=[r_ck[k], r_R[kr]], r_acc=r_acc[par],
                                               jstart=(True, False, False, False),
                                               fin=(fin if kb == kbs[-1] else None)))
                    units.append((load, usteps))
        self.run_units(units)

    def c_combine(self, ip, nd, r_nd, den, r_den, ybt, r_ybt, ystage, r_ys, tb, r_tb, cnt):
        sc = self.sc
        for half in range(2):
            d_ = den[half]
            sc.op("dve", lambda e, d_=d_, half=half: e.tensor_tensor(
                d_[:, :], nd[0][half][:, :, 64], nd[1][half][:, :, 64], ALU.add),
                [r_nd[0][half], r_nd[1][half]], [r_den[half]])
            sc.op("dve", lambda e, d_=d_, half=half: e.tensor_tensor(
                d_[:, :], d_[:, :], nd[2][half][:, :, 64], ALU.add),
                [r_den[half], r_nd[2][half]], [r_den[half]])
            sc.op("dve", lambda e, d_=d_: e.reciprocal(d_[:, :], d_[:, :]), [r_den[half]], [r_den[half]])
        lvl = int(os.environ.get("KDBG_LVL", "9"))
        if lvl < 1:
            return
        for g in range(3):
            for half in range(2):
                for qt in range(16):
                    sc.op("dve", lambda e, g=g, half=half, qt=qt: e.tensor_scalar(
                        ybt[g][:, qt, half * 64:(half + 1) * 64], nd[g][half][:, qt, 0:64],
                        den[half][:, qt:qt + 1], None, ALU.mult),
                        [r_nd[g][half], r_den[half]], [r_ybt[g]])
            if lvl < 2:
                continue
            yi = cnt["ys"] % 2
            cnt["ys"] += 1
            ys = ystage[yi]
            for bt in range(2):
                bi = cnt["t"] % 3
                cnt["t"] += 1
                tbb = tb[bi]

                def trs(e, g=g, bt=bt, tbb=tbb):
                    ins = None
                    for j in range(8):
                        ins = e.transpose(tbb[:, j * 128:(j + 1) * 128], ybt[g][:, bt * 8 + j, :], self.ident[:, :])
                    return ins
                sc.op("pe", trs, [r_ybt[g]], [r_tb[bi]])
                sc.op("dve", lambda e, ys=ys, bt=bt, tbb=tbb: e.tensor_copy(ys[:, bt * 1024:(bt + 1) * 1024], tbb[:, :]),
                      [r_tb[bi]], [r_ys[yi]])
            ch = 6 + g * 2 + ip
            if lvl < 3:
                continue
            self.dma("sp", self.yT[ch * 128:(ch + 1) * 128, :], ys[:, :], [r_ys[yi]], [])


    def phase_d(self, l, s):
        sc, ar = self.sc, self.ar
        self.phase_begin()
        PADW = 8 + S + 8
        cp = [ar.alloc([128, PADW], F32, "cp") for _ in range(2)]
        r_cp = [Res(), Res()]
        inv = [ar.alloc([128, S], F32, "inv") for _ in range(2)]
        r_inv = [Res(), Res()]
        acc = [ar.alloc([128, S], F32, "acc") for _ in range(2)]
        r_ac = [Res(), Res()]
        dpb = [ar.alloc([128, S], BF16, "dpb") for _ in range(2)]
        r_dp = [Res(), Res()]
        pw = [ar.alloc([128, 128], BF16, "pw") for _ in range(2)]
        r_pw = [Res(), Res()]
        psc = ar.alloc([128, 4], F32, "psc")
        r_psc = Res()
        for g in range(4):
            self.dma("sp", psc[:, g:g + 1], self.pool_scale[l, g * 128:(g + 1) * 128].rearrange("(e o) -> e o", o=1),
                     writes=[r_psc])
        ys = [ar.alloc([128, S], BF16, "ys") for _ in range(2)]
        r_ys = [Res(), Res()]
        bks = self.banks[0:4]
        r_bk = [Res() for _ in range(4)]
        for k in range(2):
            sc.op("dve", lambda e, k=k: e.memset(cp[k][:, 0:8], 0.0), [], [r_cp[k]])
            sc.op("dve", lambda e, k=k: e.memset(cp[k][:, 8 + S:PADW], 0.0), [], [r_cp[k]])
        nb = 0
        for g, win in enumerate((2, 4, 8, 16)):
            k = g % 2
            rad = win // 2
            eng = "pool" if g == 3 else "dve"
            self.dma("sp", cp[k][:, 8:8 + S], self.cscr[g * 128:(g + 1) * 128, :], writes=[r_cp[k]])
            self.dma("sp", inv[k][:, :], self.c_invcnt[g:g + 1, :].partition_broadcast(128), writes=[r_inv[k]])
            self.dma("pool", pw[k][:, :], self.pool_w[l, g], writes=[r_pw[k]])
            a = acc[k]
            offs = [d_ for d_ in range(-rad, rad + 1)]
            sc.op(eng, lambda e, a=a, k=k, o0=offs[0], o1=offs[1]: e.tensor_tensor(
                a[:, :], cp[k][:, 8 + o0:8 + o0 + S], cp[k][:, 8 + o1:8 + o1 + S], ALU.add), [r_cp[k]], [r_ac[k]])
            for o_ in offs[2:]:
                sc.op(eng, lambda e, a=a, k=k, o_=o_: e.tensor_tensor(
                    a[:, :], a[:, :], cp[k][:, 8 + o_:8 + o_ + S], ALU.add), [r_cp[k], r_ac[k]], [r_ac[k]])
            sc.op(eng, lambda e, a=a, k=k: e.tensor_tensor(a[:, :], a[:, :], inv[k][:, :], ALU.mult),
                  [r_ac[k], r_inv[k]], [r_ac[k]])
            sc.op(eng, lambda e, a=a, k=k: e.tensor_tensor(dpb[k][:, :], a[:, :], cp[k][:, 8:8 + S], ALU.subtract),
                  [r_ac[k], r_cp[k]], [r_dp[k]])
            for tc in range(4):
                bi = nb % 4
                nb += 1
                b = bks[bi]
                sc.op("pe", lambda e, b=b, k=k, tc=tc: e.matmul(b[:, :], pw[k][:, :], dpb[k][:, tc * 512:(tc + 1) * 512],
                                                              start=True, stop=True), [r_pw[k], r_dp[k]], [r_bk[bi]])
                sc.op("act", lambda e, b=b, k=k, tc=tc, g=g: e.activation(
                    ys[k][:, tc * 512:(tc + 1) * 512], b[:, :], AF.Copy, scale=psc[:, g:g + 1]),
                    [r_bk[bi], r_psc], [r_ys[k]])
            self.dma("sp", self.yT[(12 + g) * 128:(13 + g) * 128, :], ys[k][:, :], [r_ys[k]], [])

    def phase_e1(self, l, s):
        sc, ar = self.sc, self.ar
        self.phase_begin()
        yT = ar.alloc([128, 16, S], BF16, "yT")
        r_y = Res()
        for ch in range(16):
            self.dma("sp", yT[:, ch, :], self.yT[ch * 128:(ch + 1) * 128, :], writes=[r_y])
        wp = [ar.alloc([128, 16, 512], BF16, "wp") for _ in range(2)]
        r_wp = [Res(), Res()]
        G = [ar.alloc([128, 3, S], BF16, "G") for _ in range(2)]
        r_G = [Res(), Res()]
        tt_ = [[ar.alloc([128, 512], F32, "tt") for _ in range(3)] for _ in range(2)]
        r_tt = [[Res() for _ in range(3)] for _ in range(2)]
        ms = [ar.alloc([128, S], BF16, "ms") for _ in range(2)]
        r_ms = [Res(), Res()]
        bks = self.banks
        r_bk = [Res() for _ in range(8)]
        nb = 0
        nt = 0
        wa = self.w_proj_a[l].rearrange("(kc p) n -> p kc n", p=128)
        wb_ = self.w_proj_b[l].rearrange("(kc p) n -> p kc n", p=128)
        wc = self.w_proj_c[l].rearrange("(kc p) n -> p kc n", p=128)
        for nbk in range(4):
            w = wp[nbk % 2]
            rw = r_wp[nbk % 2]
            cs = slice(nbk * 512, (nbk + 1) * 512)
            self.dma("pool", w[:, 0:6, :], wa[:, :, cs], writes=[rw])
            self.dma("pool", w[:, 6:12, :], wb_[:, :, cs], writes=[rw])
            self.dma("pool", w[:, 12:16, :], wc[:, :, cs], writes=[rw])
            for dl in range(4):
                dmc = nbk * 4 + dl
                gk = dmc % 2
                for br in range(3):
                    self.dma("sp", G[gk][:, br, :], self.gscr[br * 2048 + dmc * 128:br * 2048 + (dmc + 1) * 128, :],
                             writes=[r_G[gk]])
                mst = ms[dmc % 2]
                rms_ = r_ms[dmc % 2]
                for tc in range(4):
                    ts = slice(tc * 512, (tc + 1) * 512)
                    tk = nt % 2
                    nt += 1
                    for br, (k0, k1) in enumerate(((0, 6), (6, 12), (12, 16))):
                        bi = nb % 8
                        nb += 1
                        b = bks[bi]

                        def mm(e, b=b, w=w, dl=dl, ts=ts, k0=k0, k1=k1):
                            ins = None
                            for kc in range(k0, k1):
                                ins = e.matmul(b[:, :], w[:, kc, dl * 128:(dl + 1) * 128], yT[:, kc, ts],
                                               start=(kc == k0), stop=(kc == k1 - 1))
                            return ins
                        sc.op("pe", mm, [rw, r_y], [r_bk[bi]])
                        t = tt_[tk][br]
                        sc.op("dve", lambda e, t=t, b=b, gk=gk, br=br, ts=ts: e.tensor_tensor(
                            t[:, :], b[:, :], G[gk][:, br, ts], ALU.mult), [r_bk[bi], r_G[gk]], [r_tt[tk][br]])
                    t0, t1, t2 = tt_[tk]
                    sc.op("pool", lambda e, t0=t0, t1=t1: e.tensor_tensor(t0[:, :], t0[:, :], t1[:, :], ALU.add),
                          [r_tt[tk][0], r_tt[tk][1]], [r_tt[tk][0]])
                    sc.op("pool", lambda e, t0=t0, t2=t2, mst=mst, ts=ts: e.tensor_tensor(
                        mst[:, ts], t0[:, :], t2[:, :], ALU.add), [r_tt[tk][0], r_tt[tk][2]], [rms_])
                self.dma("sp", self.mT[dmc * 128:(dmc + 1) * 128, :], mst[:, :], [rms_], [])

    def proj_residual(self, aT_src, nkc, wsrc, xin, xout):
        sc, ar = self.sc, self.ar
        aT = ar.alloc([128, nkc, S], BF16, "aT")
        r_a = Res()
        for ch in range(nkc):
            self.dma("sp", aT[:, ch, :], aT_src[ch * 128:(ch + 1) * 128, :], writes=[r_a])
        wo = [ar.alloc([128, nkc, 512], BF16, "wo") for _ in range(2)]
        r_wo = [Res(), Res()]
        xt = [ar.alloc([128, 512], F32, "xt") for _ in range(3)]
        r_xt = [Res() for _ in range(3)]
        xo = [ar.alloc([128, 512], F32, "xo") for _ in range(3)]
        r_xo = [Res() for _ in range(3)]
        bks = self.banks
        r_bk = [Res() for _ in range(8)]
        wv = wsrc.rearrange("(kc p) n -> p kc n", p=128)
        n = 0
        for nbk in range(4):
            w = wo[nbk % 2]
            rw = r_wo[nbk % 2]
            cs = slice(nbk * 512, (nbk + 1) * 512)
            self.dma("pool", w[:, :, :], wv[:, :, cs], writes=[rw])
            for tt in range(16):
                rs_ = slice(tt * 128, (tt + 1) * 128)
                i3 = n % 3
                bi = n % 8
                n += 1
                b = bks[bi]
                self.dma("sp", xt[i3][:, :], xin[rs_, cs], writes=[r_xt[i3]])

                def mm(e, b=b, w=w, rs_=rs_):
                    ins = None
                    for kc in range(nkc):
                        ins = e.matmul(b[:, :], aT[:, kc, rs_], w[:, kc, :], start=(kc == 0), stop=(kc == nkc - 1))
                    return ins
                sc.op("pe", mm, [r_a, rw], [r_bk[bi]])
                sc.op("dve", lambda e, b=b, i3=i3: e.tensor_tensor(xo[i3][:, :], b[:, :], xt[i3][:, :], ALU.add),
                      [r_bk[bi], r_xt[i3]], [r_xo[i3]])
                self.dma("sp", xout[rs_, cs], xo[i3][:, :], [r_xo[i3]], [])

    def phase_e2(self, l, s, xin, xout):
        self.phase_begin()
        self.proj_residual(self.mT, 16, self.w_out[l], xin, xout)

    def phase_f1(self, l, s, xsrc):
        sc, ar = self.sc, self.ar
        self.phase_begin()
        hT = ar.alloc([128, 16, S], BF16, "hT")
        r_hT = Res()
        self.norm_to_hT(xsrc, self.norm2_g[l:l + 1, :], hT, r_hT)
        wb = [ar.alloc([128, 16, 512], BF16, "wb") for _ in range(2)]
        r_wb = [Res(), Res()]
        rl = [ar.alloc([128, 512], F32, "rl") for _ in range(3)]
        r_rl = [Res() for _ in range(3)]
        ust = [ar.alloc([128, S], BF16, "ust") for _ in range(3)]
        r_us = [Res() for _ in range(3)]
        bks = self.banks[2:8]
        r_bk = [Res() for _ in range(6)]
        wv = self.w_up[l].rearrange("(kc p) n -> p kc n", p=128)
        n = 0
        for jb in range(DFF // 512):
            w = wb[jb % 2]
            rw = r_wb[jb % 2]
            self.dma("pool", w[:, :, :], wv[:, :, jb * 512:(jb + 1) * 512], writes=[rw])
            for lc in range(4):
                ffc = jb * 4 + lc
                us = ust[ffc % 3]
                rus = r_us[ffc % 3]
                for tc in range(4):
                    bi = n % 6
                    ri = n % 3
                    n += 1
                    b = bks[bi]

                    def mm(e, b=b, w=w, lc=lc, tc=tc):
                        ins = None
                        for kc in range(16):
                            ins = e.matmul(b[:, :], w[:, kc, lc * 128:(lc + 1) * 128], hT[:, kc, tc * 512:(tc + 1) * 512],
                                           start=(kc == 0), stop=(kc == 15))
                        return ins
                    sc.op("pe", mm, [r_hT, rw], [r_bk[bi]])
                    r_ = rl[ri]
                    sc.op("act", lambda e, r_=r_, b=b: e.activation(r_[:, :], b[:, :], AF.Relu), [r_bk[bi]], [r_rl[ri]])
                    eng = "dve" if n % 2 == 0 else "pool"
                    sc.op(eng, lambda e, r_=r_, us=us, tc=tc: e.tensor_tensor(
                        us[:, tc * 512:(tc + 1) * 512], r_[:, :], r_[:, :], ALU.mult), [r_rl[ri]], [rus])
                dst = bass.AP(self.us.tensor, ffc * 128, [[64 * 128, 128], [128 * 64 * 128, 16], [1, 128]])
                self.dma("sp", dst, us[:, :].rearrange("p (t q) -> p t q", t=16), [rus], [])

    def phase_f2(self, l, s, xin, xout):
        sc, ar = self.sc, self.ar
        self.phase_begin()
        NQ = 4
        wd = [ar.alloc([128, 16, 512], BF16, "wd") for _ in range(8)]
        r_wd = [Res() for _ in range(8)]
        ut = [ar.alloc([128, 64, 128], BF16, "ut") for _ in range(2)]
        r_ut = [Res(), Res()]
        xt = [ar.alloc([128, 512], F32, "xt") for _ in range(3)]
        r_xt = [Res() for _ in range(3)]
        xo = [ar.alloc([128, 512], F32, "xo") for _ in range(3)]
        r_xo = [Res() for _ in range(3)]
        bks = self.banks
        r_bk = [Res() for _ in range(8)]
        wv = self.w_down[l].rearrange("(kc p) n -> p kc n", p=128)
        n = 0
        for nbk in range(4):
            cs = slice(nbk * 512, (nbk + 1) * 512)
            wq = []
            for q in range(NQ):
                wi = (nbk * NQ + q) % 8
                self.dma("pool", wd[wi][:, :, :], wv[:, q * 16:(q + 1) * 16, cs], writes=[r_wd[wi]])
                wq.append((wd[wi], r_wd[wi]))
            for tt in range(16):
                rs_ = slice(tt * 128, (tt + 1) * 128)
                i3 = n % 3
                bi = n % 8
                u = ut[n % 2]
                ru = r_ut[n % 2]
                n += 1
                b = bks[bi]
                self.dma("sp", u[:, :, :], self.us[tt], writes=[ru])
                self.dma("sp", xt[i3][:, :], xin[rs_, cs], writes=[r_xt[i3]])

                def mm(e, b=b, wq=wq, u=u):
                    ins = None
                    for kc in range(64):
                        w = wq[kc // 16][0]
                        ins = e.matmul(b[:, :], u[:, kc, :], w[:, kc % 16, :], start=(kc == 0), stop=(kc == 63))
                    return ins
                sc.op("pe", mm, [ru] + [q_[1] for q_ in wq], [r_bk[bi]])
                sc.op("dve", lambda e, b=b, i3=i3: e.tensor_tensor(xo[i3][:, :], b[:, :], xt[i3][:, :], ALU.add),
                      [r_bk[bi], r_xt[i3]], [r_xo[i3]])
                self.dma("sp", xout[rs_, cs], xo[i3][:, :], [r_xo[i3]], [])

    def phase_final(self, s, xsrc, dst):
        sc, ar = self.sc, self.ar
        self.phase_begin()
        gbc = ar.alloc([128, D], F32, "gbc")
        r_g = Res()
        self.dma("sp", gbc[:, :], self.final_g.partition_broadcast(128), writes=[r_g])
        xt = [ar.alloc([128, D], F32, "xt") for _ in range(3)]
        r_xt = [Res() for _ in range(3)]
        junk = ar.alloc([128, D], BF16, "junk")
        r_junk = Res()
        ss = [ar.alloc([128, 1], F32, "ss") for _ in range(3)]
        r_ss = [Res() for _ in range(3)]
        xo = [ar.alloc([128, D], F32, "xo") for _ in range(3)]
        r_xo = [Res() for _ in range(3)]
        for tt in range(S // 128):
            k = tt % 3
            rs_ = slice(tt * 128, (tt + 1) * 128)
            self.dma("sp", xt[k][:, :], xsrc[rs_, :], writes=[r_xt[k]])
            sc.op("act", lambda e, k=k: e.activation(junk[:, :], xt[k][:, :], AF.Square, scale=1.0 / math.sqrt(D),
                                                     accum_out=ss[k][:, :]), [r_xt[k]], [r_junk, r_ss[k]])
            sc.op("act", lambda e, k=k: e.activation(ss[k][:, :], ss[k][:, :], AF.Sqrt, bias=EPS), [r_ss[k]], [r_ss[k]])
            sc.op("dve", lambda e, k=k: e.reciprocal(ss[k][:, :], ss[k][:, :]), [r_ss[k]], [r_ss[k]])
            sc.op("dve", lambda e, k=k: e.scalar_tensor_tensor(xo[k][:, :], xt[k][:, :], ss[k][:, 0:1], gbc[:, :],
                                                               ALU.mult, ALU.mult), [r_xt[k], r_ss[k], r_g], [r_xo[k]])
            self.dma("sp", dst[rs_, :], xo[k][:, :], [r_xo[k]], [])

    def build(self):
        nc, sc = self.nc, self.sc
        self.setup()
        done = False
        for l in range(self.n_layers):
            for s in range(self.n_seq):
                sl = slice(s * S, (s + 1) * S)
                xprev = self.x if l == 0 else self.xb
                stages = [("a", lambda: self.phase_a(l, s, xprev[sl, :])),
                          ("b", lambda: self.phase_b(l, s)),
                          ("c", lambda: self.phase_c(l, s)),
                          ("d", lambda: self.phase_d(l, s)),
                          ("e1", lambda: self.phase_e1(l, s)),
                          ("e2", lambda: self.phase_e2(l, s, xprev[sl, :], self.xa[sl, :])),
                          ("f1", lambda: self.phase_f1(l, s, self.xa[sl, :])),
                          ("f2", lambda: self.phase_f2(l, s, self.xa[sl, :], self.xb[sl, :]))]
                for name, fn in stages:
                    fn()
                    if self.stop_after == name:
                        done = True
                        break
                if done:
                    break
            if done:
                break
        if not done:
            for s in range(self.n_seq):
                sl = slice(s * S, (s + 1) * S)
                self.phase_final(s, self.xb[sl, :], self.out[sl, :])
        fin = sc.finish()
        with ExitStack() as st:
            engsem = {e: st.enter_context(nc.semaphore(f"es_{e}")) for e in ENG}
            dmasems = {e: [st.enter_context(nc.semaphore(f"ds_{e}_{i}")) for i in range(NDMASEM)]
                       for e in ("sp", "pool")}
            block = st.enter_context(nc.Block())
            sc.emit(nc, block, engsem, dmasems)
        return nc


def t5_bucket_np(rel):
    n = np.abs(rel)
    sign_off = np.where(rel > 0, 16, 0)
    thr = np.array([15, 27, 50, 91, 166, 305, 559])
    large = 8 + (n[..., None] >= thr).sum(-1)
    return sign_off + np.where(n < 8, n, large)


def make_consts():
    j = np.arange(GW)
    rel = U0 + 127 - j
    bucket = t5_bucket_np(rel)
    onehot = np.zeros((32, GW), np.float32)
    onehot[bucket, j] = 1.0
    mask = np.zeros((18, GW), np.float32)
    mask[0:6] = 1.0
    for g, (win, r) in enumerate(((128, 1), (512, 4), (2048, 16))):
        ok = ((rel % r) == 0) & (np.abs(rel) <= (win // (2 * r)) * r)
        mask[6 + 4 * g:10 + 4 * g] = ok.astype(np.float32)[None, :]
    mask[:, GW - 1] = 0.0
    ident = np.eye(128, dtype=np.float32)
    antiid = np.ascontiguousarray(ident[::-1])
    pos = np.arange(S)
    invcnt = np.zeros((4, S), np.float32)
    for g, win in enumerate((2, 4, 8, 16)):
        rad = win // 2
        lo = np.clip(pos - rad, 0, S)
        hi = np.clip(pos + rad + 1, 0, S)
        invcnt[g] = 1.0 / (hi - lo).astype(np.float32)
    return {"c_onehot": onehot, "c_mask": mask, "c_ident": ident, "c_antiid": antiid, "c_invcnt": invcnt}


_PROG_CACHE = {}


def kernel(**inputs):
    n = 8
    x = np.ascontiguousarray(np.asarray(inputs["x"], dtype=np.float32))
    B = x.shape[0]
    per = B // n
    consts = make_consts()
    shared = {}
    for k, v in inputs.items():
        if k == "x":
            continue
        a = np.ascontiguousarray(np.asarray(v, dtype=np.float32))
        if k == "final_g":
            a = a.reshape(1, D)
        shared[k] = a
    shared.update(consts)
    if "prog" not in _PROG_CACHE:
        _PROG_CACHE["prog"] = Prog().build()
    nc = _PROG_CACHE["prog"]
    in_maps = []
    for c in range(n):
        m = dict(shared)
        m["x"] = x[c * per:(c + 1) * per].reshape(per * S, D)
        in_maps.append(m)
    res = run_bass_kernel_spmd(nc, in_maps, core_ids=list(range(n)))
    outs = [r["out"].reshape(per, S, D) for r in res.results]
    return np.concatenate(outs, axis=0).astype(np.float32)
```

```python
import math
import os
from contextlib import ExitStack

import numpy as np
import concourse.bass as bass
import concourse.mybir as mybir
from concourse.bass_utils import run_bass_kernel_spmd

F32 = mybir.dt.float32
BF16 = mybir.dt.bfloat16
AF = mybir.ActivationFunctionType
ALU = mybir.AluOpType
AX = mybir.AxisListType

D = 2048
S = 2048
DEPTH = 4
NSEQ = 2
IN_COLS = 11264
DFF = 8192
EPS = 1e-6
U0 = 1920
RW = 3968
GW = 4096
ENG = ("pe", "act", "dve", "pool", "sp")
NDMASEM = 20


class Res:
    __slots__ = ("name", "w", "r", "rd")

    def __init__(self, name=""):
        self.name = name
        self.w = None
        self.r = {}
        self.rd = []


class Op:
    __slots__ = ("eng", "fn", "deps", "need_inc", "idx", "dma", "sem", "semval", "semprev", "phase")

    def __init__(self, eng, fn, dma):
        self.eng = eng
        self.fn = fn
        self.dma = dma
        self.deps = ()
        self.need_inc = False
        self.idx = 0
        self.sem = None
        self.semval = 0
        self.semprev = 0


class Sched:
    def __init__(self):
        self.ops = {e: [] for e in ENG}
        self.since_barrier = []
        self.last = {e: None for e in ENG}
        self.barrier_deps = {e: [] for e in ENG}
        self.phase = "setup"
        self.scopes = False

    def op(self, eng, fn, reads=(), writes=(), dma=False):
        o = Op(eng, fn, dma)
        o.phase = self.phase
        deps = set(self.barrier_deps[eng])
        self.barrier_deps[eng] = []
        for r in reads:
            if r.w is not None:
                deps.add(r.w)
        for w in writes:
            if w.w is not None:
                deps.add(w.w)
            deps.update(w.r.values())
            deps.update(w.rd)
        dl = []
        for d in deps:
            if d is o:
                continue
            if (not d.dma) and (not dma) and d.eng == "pe" and eng == "pe":
                continue
            dl.append(d)
            if not d.dma:
                d.need_inc = True
        o.deps = dl
        for r in reads:
            if dma:
                r.rd.append(o)
            else:
                r.r[eng] = o
        for w in writes:
            w.w = o
            w.r = {}
            w.rd = []
        self.ops[eng].append(o)
        self.last[eng] = o
        if dma:
            self.since_barrier.append(o)
        return o

    def barrier(self):
        deps = list(self.since_barrier)
        for e in ENG:
            if self.last[e] is not None:
                deps.append(self.last[e])
        self.since_barrier = []
        for e in ENG:
            self.barrier_deps[e] = list(deps)

    def finish(self):
        self.barrier()
        return self.op("sp", None)

    def emit(self, nc, block, engsem, dmasems):
        for e in ENG:
            c = 0
            nd = 0
            for o in self.ops[e]:
                if o.dma:
                    k = nd % NDMASEM
                    o.sem = dmasems[e][k]
                    o.semprev = 16 * (nd // NDMASEM)
                    o.semval = o.semprev + 16
                    nd += 1
                elif o.need_inc:
                    c += 1
                    o.idx = c
        handles = {"pe": block.tensor, "act": block.scalar, "dve": block.vector,
                   "pool": block.gpsimd, "sp": block.sync}

        def make(e):
            def run(eng):
                seen = {}
                cur = None
                for o in self.ops[e]:
                    if self.scopes and (cur is None or o.phase != cur[0]):
                        if cur is not None:
                            nc.leave_named_scope(cur[0], cur[1], False)
                        sid, _ = nc.enter_named_scope(o.phase, False)
                        cur = (o.phase, sid)
                    if self.scopes:
                        cur_name = cur[0]
                    waits = {}
                    for d in o.deps:
                        if d.dma:
                            key, val = d.sem, d.semval
                        else:
                            key, val = engsem[d.eng], d.idx
                        kid = id(key)
                        if kid not in waits or waits[kid][1] < val:
                            waits[kid] = (key, val)
                    if o.dma and o.semprev > 0:
                        kid = id(o.sem)
                        if kid not in waits or waits[kid][1] < o.semprev:
                            waits[kid] = (o.sem, o.semprev)
                    for kid, (key, val) in waits.items():
                        if seen.get(kid, 0) >= val:
                            continue
                        eng.wait_ge(key, val)
                        seen[kid] = val
                    if o.fn is None:
                        continue
                    ins = o.fn(eng)
                    if o.dma:
                        ins.then_inc(o.sem, 16)
                    elif o.need_inc:
                        ins.then_inc(engsem[e], 1)
                if self.scopes and cur is not None:
                    nc.leave_named_scope(cur[0], cur[1], False)
            return run

        for e in ENG:
            handles[e](make(e))


class Arena:
    def __init__(self, nc, base, limit):
        self.nc = nc
        self.base = base
        self.cur = base
        self.limit = limit
        self.n = 0

    def reset(self):
        self.cur = self.base

    def alloc(self, shape, dtype, name="t"):
        nbytes = int(np.prod(shape[1:])) * (4 if dtype == F32 else 2)
        nbytes = (nbytes + 63) // 64 * 64
        off = self.cur
        self.cur += nbytes
        assert self.cur <= self.limit, f"arena overflow {self.cur} > {self.limit} ({name})"
        self.n += 1
        return self.nc.alloc_sbuf_tensor_at(f"{name}_{self.n}", list(shape), dtype, offset=off)


def rot(lst, i):
    return lst[i % len(lst)]


class Prog:
    def __init__(self, n_layers=DEPTH, n_seq=NSEQ, taps=False, stop_after=None):
        self.n_layers = n_layers
        self.n_seq = n_seq
        self.taps = taps
        self.stop_after = stop_after
        nc = bass.Bass("TRN2", target_bir_lowering=False)
        self.nc = nc
        self.sc = Sched()
        T = NSEQ * S

        def din(name, shape):
            return nc.dram_tensor(name, list(shape), F32, kind="ExternalInput").ap()

        self.x = din("x", [T, D])
        self.table = din("rel_bias_table", [32, 18])
        self.norm1_g = din("norm1_g", [DEPTH, D])
        self.w_in = din("w_in", [DEPTH, D, IN_COLS])
        self.lq1 = din("lambda_q1", [DEPTH, 64])
        self.lk1 = din("lambda_k1", [DEPTH, 64])
        self.lq2 = din("lambda_q2", [DEPTH, 64])
        self.lk2 = din("lambda_k2", [DEPTH, 64])
        self.subln_g = din("subln_g", [DEPTH, 128])
        self.pool_w = din("pool_w", [DEPTH, 4, 128, 128])
        self.pool_scale = din("pool_scale", [DEPTH, 512])
        self.w_proj_a = din("w_proj_a", [DEPTH, 768, D])
        self.w_proj_b = din("w_proj_b", [DEPTH, 768, D])
        self.w_proj_c = din("w_proj_c", [DEPTH, 512, D])
        self.w_out = din("w_out", [DEPTH, D, D])
        self.norm2_g = din("norm2_g", [DEPTH, D])
        self.w_up = din("w_up", [DEPTH, D, DFF])
        self.w_down = din("w_down", [DEPTH, DFF, D])
        self.final_g = din("final_g", [1, D])
        self.c_onehot = din("c_onehot", [32, GW])
        self.c_mask = din("c_mask", [18, GW])
        self.c_ident = din("c_ident", [128, 128])
        self.c_antiid = din("c_antiid", [128, 128])
        self.c_invcnt = din("c_invcnt", [4, S])

        self.out = nc.dram_tensor("out", [T, D], F32, kind="ExternalOutput").ap()

        def scr(name, shape, dt):
            kind = "ExternalOutput" if taps else "Internal"
            return nc.dram_tensor(name, list(shape), dt, kind=kind).ap()

        self.grow = scr("s_grow", [18, GW], BF16)
        self.rscr = scr("s_r", [18, 128, RW], BF16)
        self.qk = scr("s_qk", [24 * 128, S], BF16)
        self.va = scr("s_va", [S, 768], BF16)
        self.vb = scr("s_vb", [S, 768], BF16)
        self.cscr = scr("s_c", [512, S], F32)
        self.gscr = scr("s_g", [6144, S], BF16)
        self.yT = scr("s_yT", [16 * 128, S], BF16)
        self.mT = scr("s_mT", [16 * 128, S], BF16)
        self.xa = scr("s_xa", [T, D], F32)
        self.xb = scr("s_xb", [T, D], F32)
        self.us = scr("s_u", [16, 128, 64, 128], BF16)

        self.ar = Arena(nc, 20480, 229344)
        self.ident = self.ar.alloc([128, 128], BF16, "ident")
        self.antiid = self.ar.alloc([128, 128], BF16, "antiid")
        self.ar.base = self.ar.cur
        self.banks = [nc.alloc_psum_tensor(f"bank{i}", [128, 512], F32) for i in range(8)]

    def dma(self, eng, out, in_, reads=(), writes=()):
        return self.sc.op(eng, lambda e: e.dma_start(out=out, in_=in_), reads, writes, dma=True)

    def phase_begin(self, name=None):
        self.sc.barrier()
        self.ar.reset()
        if name is not None:
            self.sc.phase = name

    def setup(self):
        sc, ar = self.sc, self.ar
        self.phase_begin()
        r_const = Res("const")
        identf = ar.alloc([128, 128], F32, "identf")
        tab = ar.alloc([32, 18], F32, "tab")
        oh = ar.alloc([32, GW], F32, "oh")
        msk = ar.alloc([18, GW], F32, "msk")
        gf = ar.alloc([18, GW], F32, "gf")
        gb = ar.alloc([18, GW], BF16, "gb")
        r_in = Res()
        self.dma("pool", self.ident[:], self.c_ident, writes=[r_const])
        self.dma("pool", self.antiid[:], self.c_antiid, writes=[r_const])
        self.dma("sp", tab[:], self.table, writes=[r_in])
        self.dma("sp", oh[:], self.c_onehot, writes=[r_in])
        self.dma("sp", msk[:], self.c_mask, writes=[r_in])
        r_gf = Res()
        for i in range(GW // 512):
            b = self.banks[i % 4]
            rb = Res()
            sc.op("pe", lambda e, b=b, i=i: e.matmul(b[0:18, :], tab[:, :], oh[:, i * 512:(i + 1) * 512],
                                                     start=True, stop=True), [r_in], [rb])
            sc.op("act", lambda e, b=b, i=i: e.activation(gf[:, i * 512:(i + 1) * 512], b[0:18, :], AF.Exp),
                  [rb], [r_gf])
        r_gb = Res()
        sc.op("dve", lambda e: e.tensor_tensor(gb[:, :], gf[:, :], msk[:, :], ALU.mult), [r_gf, r_in], [r_gb])
        r_grow = Res()
        self.dma("sp", self.grow, gb[:, :], [r_gb], [r_grow])
        t1 = [ar.alloc([128, RW], BF16, "t1") for _ in range(2)]
        rt = [ar.alloc([128, RW], BF16, "rt") for _ in range(2)]
        r_t1 = [Res(), Res()]
        r_rt = [Res(), Res()]
        r_bank = [Res() for _ in range(8)]
        nb = 0
        for h in range(18):
            src = bass.AP(self.grow.tensor, h * GW, [[1, 128], [1, RW]])
            self.dma("sp", t1[h % 2][:, :], src, [r_grow], [r_t1[h % 2]])
            for i in range((RW + 511) // 512):
                w = min(512, RW - i * 512)
                bi = nb % 8
                nb += 1
                b = self.banks[bi]
                sc.op("pe", lambda e, b=b, i=i, w=w, h=h: e.matmul(
                    b[:, 0:w], self.antiid[:, :], t1[h % 2][:, i * 512:i * 512 + w], start=True, stop=True),
                    [r_t1[h % 2], r_const], [r_bank[bi]])
                eng = "act" if i % 2 == 0 else "dve"
                if eng == "act":
                    sc.op("act", lambda e, b=b, i=i, w=w, h=h: e.copy(rt[h % 2][:, i * 512:i * 512 + w], b[:, 0:w]),
                          [r_bank[bi]], [r_rt[h % 2]])
                else:
                    sc.op("dve", lambda e, b=b, i=i, w=w, h=h: e.tensor_copy(rt[h % 2][:, i * 512:i * 512 + w], b[:, 0:w]),
                          [r_bank[bi]], [r_rt[h % 2]])
            self.dma("sp", self.rscr[h], rt[h % 2][:, :], [r_rt[h % 2]], [])

    def norm_to_hT(self, xsrc, grow_ap, hT, r_hT):
        sc, ar = self.sc, self.ar
        gbc = ar.alloc([128, D], F32, "gbc")
        r_g = Res()
        self.dma("sp", gbc[:, :], grow_ap.partition_broadcast(128), writes=[r_g])
        xt = [ar.alloc([128, D], F32, "xt") for _ in range(2)]
        r_xt = [Res(), Res()]
        junk = ar.alloc([128, D], BF16, "junk")
        r_junk = Res()
        ss = [ar.alloc([128, 1], F32, "ss") for _ in range(2)]
        r_ss = [Res(), Res()]
        xn = [ar.alloc([128, D], BF16, "xn") for _ in range(2)]
        r_xn = [Res(), Res()]
        r_b = [Res(), Res()]
        for tt in range(S // 128):
            k = tt % 2
            self.dma("sp", xt[k][:, :], xsrc[tt * 128:(tt + 1) * 128, :], writes=[r_xt[k]])
            sc.op("act", lambda e, k=k: e.activation(junk[:, :], xt[k][:, :], AF.Square, scale=1.0 / math.sqrt(D),
                                                     accum_out=ss[k][:, :]),
                  [r_xt[k]], [r_junk, r_ss[k]])
            sc.op("act", lambda e, k=k: e.activation(ss[k][:, :], ss[k][:, :], AF.Sqrt, bias=EPS),
                  [r_ss[k]], [r_ss[k]])
            sc.op("dve", lambda e, k=k: e.reciprocal(ss[k][:, :], ss[k][:, :]),
                  [r_ss[k]], [r_ss[k]])
            sc.op("dve", lambda e, k=k: e.scalar_tensor_tensor(xn[k][:, :], xt[k][:, :], ss[k][:, 0:1], gbc[:, :],
                                                               ALU.mult, ALU.mult),
                  [r_xt[k], r_ss[k], r_g], [r_xn[k]])
            for hf in range(2):
                pb = self.banks[hf].ap().bitcast(BF16)

                def tr(e, k=k, hf=hf, pb=pb):
                    ins = None
                    for j in range(8):
                        kc = hf * 8 + j
                        ins = e.transpose(pb[:, j * 128:(j + 1) * 128], xn[k][:, kc * 128:(kc + 1) * 128],
                                          self.ident[:, :])
                    return ins
                sc.op("pe", tr, [r_xn[k]], [r_b[hf]])
                dst = hT[:, hf * 8:(hf + 1) * 8, tt * 128:(tt + 1) * 128]
                srcp = pb.rearrange("p (j q) -> p j q", j=8)
                if hf == 0:
                    sc.op("act", lambda e, dst=dst, srcp=srcp: e.copy(dst, srcp), [r_b[hf]], [r_hT])
                else:
                    sc.op("dve", lambda e, dst=dst, srcp=srcp: e.tensor_copy(dst, srcp), [r_b[hf]], [r_hT])

    def phase_a(self, l, s, xsrc):
        sc, ar = self.sc, self.ar
        self.phase_begin("A")
        hT = ar.alloc([128, 16, S], BF16, "hT")
        r_hT = Res()
        self.norm_to_hT(xsrc, self.norm1_g[l:l + 1, :], hT, r_hT)
        wb = [ar.alloc([128, 16, 512], BF16, "wb") for _ in range(2)]
        r_wb = [Res(), Res()]
        stg = [ar.alloc([128, S], BF16, "stg") for _ in range(3)]
        r_stg = [Res() for _ in range(3)]
        stgv = [ar.alloc([128, 512], BF16, "stgv") for _ in range(3)]
        r_stgv = [Res() for _ in range(3)]
        stgc = [ar.alloc([128, S], F32, "stgc") for _ in range(2)]
        r_stgc = [Res() for _ in range(2)]
        bks = self.banks[2:8]
        r_bk = [Res() for _ in range(6)]
        nbk = 0
        nstg = 0
        nstgv = 0
        nstgc = 0
        nev = 0
        wv = self.w_in[l].rearrange("(kc p) n -> p kc n", p=128)
        for j in range(IN_COLS // 512):
            w = wb[j % 2]
            rw = r_wb[j % 2]
            self.dma("pool", w[:, :, :], wv[:, :, j * 512:(j + 1) * 512], writes=[rw])
            lc = 0
            while lc < 4:
                cc = 4 * j + lc
                zform = (12 <= cc < 18) or (30 <= cc < 36)
                if zform:
                    n = 1
                    while lc + n < 4 and ((12 <= cc + n < 18) or (30 <= cc + n < 36)):
                        n += 1
                    ncol = n * 128
                    if cc < 18:
                        vdst, vc0 = self.va, (cc - 12) * 128
                    else:
                        vdst, vc0 = self.vb, (cc - 30) * 128
                    for tt in range(16):
                        bi = nbk % 6
                        nbk += 1
                        b = bks[bi]

                        def mm(e, b=b, w=w, tt=tt, lc=lc, ncol=ncol):
                            ins = None
                            for kc in range(16):
                                ins = e.matmul(b[:, 0:ncol], hT[:, kc, tt * 128:(tt + 1) * 128],
                                               w[:, kc, lc * 128:lc * 128 + ncol], start=(kc == 0), stop=(kc == 15))
                            return ins
                        sc.op("pe", mm, [r_hT, rw], [r_bk[bi]])
                        si = nstgv % 3
                        nstgv += 1
                        st = stgv[si]
                        if nev % 2 == 0:
                            sc.op("act", lambda e, st=st, b=b, ncol=ncol: e.copy(st[:, 0:ncol], b[:, 0:ncol]),
                                  [r_bk[bi]], [r_stgv[si]])
                        else:
                            sc.op("dve", lambda e, st=st, b=b, ncol=ncol: e.tensor_copy(st[:, 0:ncol], b[:, 0:ncol]),
                                  [r_bk[bi]], [r_stgv[si]])
                        nev += 1
                        self.dma("sp", vdst[tt * 128:(tt + 1) * 128, vc0:vc0 + ncol], st[:, 0:ncol],
                                 [r_stgv[si]], [])
                    lc += n
                    continue
                if cc < 12:
                    kind, dst = "qk", self.qk[cc * 128:(cc + 1) * 128, :]
                elif cc < 30:
                    kind, dst = "qk", self.qk[(cc - 6) * 128:(cc - 5) * 128, :]
                elif cc < 40:
                    kind, dst = "c", self.cscr[(cc - 36) * 128:(cc - 35) * 128, :]
                else:
                    kind, dst = "g", self.gscr[(cc - 40) * 128:(cc - 39) * 128, :]
                if kind == "c":
                    si = nstgc % 2
                    nstgc += 1
                    st, rs = stgc[si], r_stgc[si]
                else:
                    si = nstg % 3
                    nstg += 1
                    st, rs = stg[si], r_stg[si]
                for tc in range(4):
                    bi = nbk % 6
                    nbk += 1
                    b = bks[bi]

                    def mm(e, b=b, w=w, tc=tc, lc=lc):
                        ins = None
                        for kc in range(16):
                            ins = e.matmul(b[:, :], w[:, kc, lc * 128:(lc + 1) * 128],
                                           hT[:, kc, tc * 512:(tc + 1) * 512], start=(kc == 0), stop=(kc == 15))
                        return ins
                    sc.op("pe", mm, [r_hT, rw], [r_bk[bi]])
                    o = st[:, tc * 512:(tc + 1) * 512]
                    if kind == "g":
                        sc.op("act", lambda e, o=o, b=b: e.activation(o, b[:, :], AF.Sigmoid), [r_bk[bi]], [rs])
                    elif nev % 2 == 0:
                        sc.op("act", lambda e, o=o, b=b: e.copy(o, b[:, :]), [r_bk[bi]], [rs])
                        nev += 1
                    else:
                        sc.op("dve", lambda e, o=o, b=b: e.tensor_copy(o, b[:, :]), [r_bk[bi]], [rs])
                        nev += 1
                self.dma("sp", dst, st[:, :], [rs], [])
                lc += 1


    def attn_run(self, steps, nst=3, la=2):
        sc = self.sc
        stb = self.banks[0:nst]
        r_st = [Res() for _ in range(nst)]
        Et = [self.ar.alloc([128, 512], BF16, "E") for _ in range(nst)]
        Pt = [self.ar.alloc([128, 512], BF16, "P") for _ in range(nst)]
        r_E = [Res() for _ in range(nst)]
        r_P = [Res() for _ in range(nst)]
        self.pending = []
        self.cur_gid = 0

        def flush(maxgid=None, count=None):
            n = 0
            while self.pending:
                if maxgid is not None and self.pending[0][0] > maxgid:
                    break
                if count is not None and n >= count:
                    break
                _, fn = self.pending.pop(0)
                fn()
                n += 1

        def emit_st(st, i):
            b = stb[i % nst]
            hs = slice(st["half"] * 64, st["half"] * 64 + 64)
            kb, qc = st["kb"], st["qc"]
            qT, kT, R = st["qT"], st["kT"], st["R"]
            sc.op("pe", lambda e: e.matmul(b[:, :], kT[hs, kb * 128:(kb + 1) * 128], qT[hs, qc * 512:(qc + 1) * 512],
                                           start=True, stop=True), st["r_in"], [r_st[i % nst]])
            E = Et[i % nst]
            sc.op("act", lambda e: e.activation(E[:, :], b[:, :], AF.Exp, scale=0.125), [r_st[i % nst]], [r_E[i % nst]])
            P = Pt[i % nst]
            s0 = U0 - (kb * 128 - qc * 512)
            sc.op("dve", lambda e: e.tensor_tensor(P[:, :], E[:, :], R[:, s0:s0 + 512], ALU.mult),
                  [r_E[i % nst]] + st["r_in"], [r_P[i % nst]])
            flush(count=2)

        def emit_av(st, i):
            P = Pt[i % nst]
            v = st["V"](st["kb"])
            accs = st["acc"]
            first, last = st["first"], st["last"]
            if first:
                flush(maxgid=st["gid"] - 2)

            def f(e):
                ins = None
                for j in range(4):
                    ins = e.matmul(accs[j], P[:, j * 128:(j + 1) * 128], v, start=(first and st["jstart"][j]),
                                   stop=last, skip_group_check=True)
                return ins
            sc.op("pe", f, [r_P[i % nst]] + st["r_in"], [st["r_acc"]])
            if last and st["fin"] is not None:
                self.cur_gid = st["gid"]
                st["fin"]()
            if st.get("post") is not None:
                st["post"]()

        n = len(steps)
        for i in range(n + la):
            if i < n:
                emit_st(steps[i], i)
            if i >= la:
                emit_av(steps[i - la], i - la)
        flush()

    def defer(self, fn):
        self.pending.append((self.cur_gid, fn))

    def phase_b(self, l, s):
        sc, ar = self.sc, self.ar
        self.phase_begin("B")
        lam_init = 0.8 - 0.6 * math.exp(-0.3 * l)
        lv = [ar.alloc([128, 64], F32, "lv") for _ in range(4)]
        r_lv = Res()
        for t, src in zip(lv, (self.lq1, self.lk1, self.lq2, self.lk2)):
            self.dma("sp", t[:, :], src[l:l + 1, :].partition_broadcast(128), writes=[r_lv])
        sm = ar.alloc([128, 8], F32, "sm")
        r_sm = Res()
        pr = ar.alloc([128, 64], F32, "pr")
        r_pr = Res()
        for i in range(2):
            sc.op("dve", lambda e, i=i: e.tensor_tensor(pr[:, :], lv[2 * i][:, :], lv[2 * i + 1][:, :], ALU.mult),
                  [r_lv], [r_pr])
            sc.op("dve", lambda e, i=i: e.reduce_sum(sm[:, i:i + 1], pr[:, :], AX.X), [r_pr], [r_sm])
        sc.op("act", lambda e: e.activation(sm[:, 2:4], sm[:, 0:2], AF.Exp), [r_sm], [r_sm])
        sc.op("dve", lambda e: e.tensor_tensor(sm[:, 4:5], sm[:, 3:4], sm[:, 2:3], ALU.subtract), [r_sm], [r_sm])
        sc.op("dve", lambda e: e.tensor_scalar_add(sm[:, 5:6], sm[:, 4:5], -lam_init), [r_sm], [r_sm])
        neglam = sm[:, 5:6]
        gsub = ar.alloc([128, 1], F32, "gsub")
        r_gs = Res()
        self.dma("sp", gsub[:, :], self.subln_g[l].rearrange("(e o) -> e o", o=1), writes=[r_gs])
        sc.op("dve", lambda e: e.tensor_scalar_mul(gsub[:, :], gsub[:, :], 1.0 - lam_init), [r_gs], [r_gs])
        qT = [ar.alloc([128, S], BF16, "qT") for _ in range(2)]
        kT = [ar.alloc([128, S], BF16, "kT") for _ in range(2)]
        V = [ar.alloc([128, 16, 130], BF16, "V") for _ in range(2)]
        R = [ar.alloc([128, RW], BF16, "R") for _ in range(2)]
        r_hd = [Res(), Res()]
        for k in range(2):
            sc.op("dve", lambda e, k=k: e.memset(V[k][:, :, 128:129], 1.0), [], [r_hd[k]])
        o0 = ar.alloc([128, 16, 128], F32, "o0")
        r_o0 = Res()
        ot = [ar.alloc([128, 4, 128], F32, "ot") for _ in range(2)]
        r_ot = [Res(), Res()]
        jk = ar.alloc([128, 4, 128], F32, "jk")
        r_jk = Res()
        st_ = [ar.alloc([128, 16], F32, "st") for _ in range(4)]
        r_stt = [Res() for _ in range(4)]
        ybf = [ar.alloc([128, 4, 128], BF16, "ybf") for _ in range(2)]
        r_ybf = [Res(), Res()]
        ystage = [ar.alloc([128, S], BF16, "ystage") for _ in range(2)]
        r_ys = [Res(), Res()]
        tb = self.banks[7].ap().bitcast(BF16)
        r_tb = Res()
        accb = [(self.banks[3], self.banks[4]), (self.banks[5], self.banks[6])]
        r_acc = [Res(), Res()]
        cnt = {"fin": 0}
        gcount = 0
        units = []
        for h in range(6):
            k = h % 2

            def load(h=h, k=k):
                self.dma("sp", qT[k][:, :], self.qk[h * 128:(h + 1) * 128, :], writes=[r_hd[k]])
                self.dma("sp", kT[k][:, :], self.qk[(6 + h) * 128:(7 + h) * 128, :], writes=[r_hd[k]])
                self.dma("sp", V[k][:, :, 0:128],
                         self.va[:, h * 128:(h + 1) * 128].rearrange("(kb p) e -> p kb e", p=128), writes=[r_hd[k]])
                self.dma("sp", R[k][:, :], self.rscr[h], writes=[r_hd[k]])
            usteps = []
            for c in range(2):
                for qc in range(4):
                    par = gcount % 2
                    gid = gcount
                    gcount += 1
                    banks = accb[par]
                    accs = [banks[j // 2][:, (j % 2) * 129:(j % 2) * 129 + 129] for j in range(4)]

                    def fin(h=h, c=c, qc=qc, banks=banks, par=par):
                        fi = cnt["fin"]
                        cnt["fin"] += 1
                        sm_ = st_[fi % 4]
                        rs = r_stt[fi % 4]
                        o = ot[fi % 2]
                        ro = r_ot[fi % 2]
                        for b2 in range(2):
                            bv = banks[b2][:, 0:258].rearrange("p (j e) -> p j e", j=2)
                            rc = sm_[:, 2 * b2:2 * b2 + 2]
                            self.defer(lambda bv=bv, rc=rc: sc.op(
                                "dve", lambda e: e.reciprocal(rc, bv[:, :, 128]), [r_acc[par]], [rs]))
                            if c == 0:
                                dst = o0[:, qc * 4 + 2 * b2:qc * 4 + 2 * b2 + 2, :]
                                self.defer(lambda bv=bv, rc=rc, dst=dst: sc.op(
                                    "dve", lambda e: e.tensor_tensor(
                                        dst, bv[:, :, 0:128], rc.unsqueeze(2).to_broadcast([128, 2, 128]), ALU.mult),
                                    [r_acc[par], rs], [r_o0]))
                            else:
                                rcn = sm_[:, 4 + 2 * b2:4 + 2 * b2 + 2]
                                self.defer(lambda rc=rc, rcn=rcn: sc.op(
                                    "dve", lambda e: e.tensor_scalar(rcn, rc, neglam, None, ALU.mult), [rs, r_sm], [rs]))
                                dst = o[:, 2 * b2:2 * b2 + 2, :]
                                self.defer(lambda bv=bv, rcn=rcn, dst=dst: sc.op(
                                    "dve", lambda e: e.tensor_tensor(
                                        dst, bv[:, :, 0:128], rcn.unsqueeze(2).to_broadcast([128, 2, 128]), ALU.mult),
                                    [r_acc[par], rs], [ro]))
                        if c == 0:
                            return
                        yb_ = ybf[fi % 2]
                        ry = r_ybf[fi % 2]
                        ys = ystage[h % 2]
                        rys = r_ys[h % 2]
                        self.defer(lambda: sc.op("dve", lambda e: e.tensor_tensor(
                            o[:, :, :], o[:, :, :], o0[:, qc * 4:qc * 4 + 4, :], ALU.add), [ro, r_o0], [ro]))
                        self.defer(lambda: sc.op("dve", lambda e: e.tensor_tensor(
                            jk[:, :, :], o[:, :, :], o[:, :, :], ALU.mult), [ro], [r_jk]))
                        self.defer(lambda: sc.op("dve", lambda e: e.reduce_sum(sm_[:, 8:12], jk[:, :, :], AX.X),
                                                 [r_jk], [rs]))
                        self.defer(lambda: sc.op("act", lambda e: e.activation(
                            sm_[:, 8:12], sm_[:, 8:12], AF.Ln, bias=EPS, scale=1.0 / 128.0), [rs], [rs]))
                        self.defer(lambda: sc.op("act", lambda e: e.activation(
                            sm_[:, 12:16], sm_[:, 8:12], AF.Exp, scale=-0.5), [rs], [rs]))
                        self.defer(lambda: sc.op("dve", lambda e: e.tensor_tensor(
                            yb_[:, :, :], o[:, :, :], sm_[:, 12:16].unsqueeze(2).to_broadcast([128, 4, 128]), ALU.mult),
                            [ro, rs], [ry]))

                        def trs(e):
                            ins = None
                            for j in range(4):
                                ins = e.transpose(tb[:, j * 128:(j + 1) * 128], yb_[:, j, :], self.ident[:, :])
                            return ins
                        self.defer(lambda: sc.op("pe", trs, [ry], [r_tb]))
                        self.defer(lambda: sc.op("dve", lambda e: e.tensor_scalar(
                            ys[:, qc * 512:(qc + 1) * 512], tb[:, 0:512], gsub[:, 0:1], None, ALU.mult),
                            [r_tb, r_gs], [rys]))
                        if qc == 3:
                            self.defer(lambda: self.dma("sp", self.yT[h * 128:(h + 1) * 128, :], ys[:, :], [rys], []))

                    for kb in range(16):
                        usteps.append(dict(qT=qT[k], kT=kT[k], half=c, R=R[k],
                                           V=(lambda kb, k=k: V[k][:, kb, 0:129]), E=128, qc=qc, kb=kb,
                                           first=(kb == 0), last=(kb == 15), acc=accs, r_in=[r_hd[k]],
                                           jstart=(True, False, True, False), gid=gid,
                                           r_acc=r_acc[par], fin=(fin if kb == 15 else None)))
            units.append((load, usteps))
        self.run_units(units)

    def run_units(self, units):
        steps = []
        for u, (load, usteps) in enumerate(units):
            if u < 2:
                load()
            if u + 2 < len(units):
                usteps[-1]["post"] = units[u + 2][0]
            steps.extend(usteps)
        self.attn_run(steps)

    def phase_c(self, l, s):
        sc, ar = self.sc, self.ar
        self.phase_begin("C")
        qT = [ar.alloc([128, S], BF16, "qT") for _ in range(2)]
        kT = [ar.alloc([128, S], BF16, "kT") for _ in range(2)]
        V = [ar.alloc([128, 16, 132], BF16, "V") for _ in range(2)]
        R = [ar.alloc([128, RW], BF16, "R") for _ in range(2)]
        r_ck = [Res(), Res()]
        r_R = [Res(), Res()]
        for k in range(2):
            sc.op("dve", lambda e, k=k: e.memset(V[k][:, :, 64:65], 1.0), [], [r_ck[k]])
            sc.op("dve", lambda e, k=k: e.memset(V[k][:, :, 130:131], 1.0), [], [r_ck[k]])
        nd = [[ar.alloc([128, 16, 65], F32, "nd") for _ in range(2)] for _ in range(3)]
        r_nd = [[Res() for _ in range(2)] for _ in range(3)]
        den = [ar.alloc([128, 16], F32, "den") for _ in range(2)]
        r_den = [Res(), Res()]
        ybt = [ar.alloc([128, 16, 128], BF16, "ybt") for _ in range(3)]
        r_ybt = [Res() for _ in range(3)]
        ystage = [ar.alloc([128, S], BF16, "ystage") for _ in range(2)]
        r_ys = [Res(), Res()]
        tb = [self.banks[i].ap().bitcast(BF16) for i in (5, 6, 7)]
        r_tb = [Res() for _ in range(3)]
        accb = [self.banks[3], self.banks[4]]
        r_acc = [Res(), Res()]
        cnt = {"t": 0, "ys": 0}
        steps = []
        gcount = 0
        nload = 0
        nR = 0
        groups = ((128, 1), (512, 4), (2048, 16))
        units = []
        for ip in range(2):
            for g, (win, r) in enumerate(groups):
                k = nload % 2
                nload += 1
                ch = g * 2 + ip
                for half in range(2):
                    hh = g * 4 + 2 * ip + half
                    kr = nR % 2
                    nR += 1

                    def load(k=k, ch=ch, half=half, hh=hh, kr=kr, g=g, ip=ip):
                        if half == 0:
                            self.dma("sp", qT[k][:, :], self.qk[(12 + ch) * 128:(13 + ch) * 128, :], writes=[r_ck[k]])
                            self.dma("sp", kT[k][:, :], self.qk[(18 + ch) * 128:(19 + ch) * 128, :], writes=[r_ck[k]])
                            for hf in range(2):
                                h2 = g * 4 + 2 * ip + hf
                                self.dma("sp", V[k][:, :, hf * 66:hf * 66 + 64],
                                         self.vb[:, h2 * 64:(h2 + 1) * 64].rearrange("(kb p) e -> p kb e", p=128),
                                         writes=[r_ck[k]])
                        self.dma("sp", R[kr][:, :], self.rscr[6 + hh], writes=[r_R[kr]])
                    usteps = []
                    span = (win // (2 * r)) * r
                    for qc in range(4):
                        q0 = qc * 512
                        kb_lo = max(0, (q0 - span) // 128)
                        kb_hi = min(15, (q0 + 511 + span) // 128)
                        kbs = list(range(kb_lo, kb_hi + 1))
                        par = gcount % 2
                        gid = gcount
                        gcount += 1
                        bank = accb[par]
                        accs = [bank[:, j * 65:j * 65 + 65] for j in range(4)]

                        def fin(g=g, half=half, qc=qc, bank=bank, par=par, ip=ip):
                            self.defer(lambda: sc.op("dve", lambda e: e.tensor_copy(
                                nd[g][half][:, qc * 4:(qc + 1) * 4, :],
                                bank[:, 0:260].rearrange("p (j e) -> p j e", j=4)),
                                [r_acc[par]], [r_nd[g][half]]))
                            if g == 2 and half == 1 and qc == 3 and not os.environ.get('KDBG_NOCOMB'):
                                self.c_combine(ip, nd, r_nd, den, r_den, ybt, r_ybt, ystage, r_ys, tb, r_tb, cnt)

                        for kb in kbs:
                            usteps.append(dict(qT=qT[k], kT=kT[k], half=half, R=R[kr],
                                               V=(lambda kb, k=k, half=half: V[k][:, kb, half * 66:half * 66 + 65]),
                                               E=64, qc=qc, kb=kb, first=(kb == kbs[0]), last=(kb == kbs[-1]),
                                               acc=accs, r_in=[r_ck[k], r_R[kr]], r_acc=r_acc[par],
                                               jstart=(True, False, False, False), gid=gid,
                                               fin=(fin if kb == kbs[-1] else None)))
                    units.append((load, usteps))
        self.run_units(units)

    def c_combine(self, ip, nd, r_nd, den, r_den, ybt, r_ybt, ystage, r_ys, tb, r_tb, cnt):
        sc = self.sc
        for half in range(2):
            d_ = den[half]
            self.defer(lambda d_=d_, half=half: sc.op("dve", lambda e: e.tensor_tensor(
                d_[:, :], nd[0][half][:, :, 64], nd[1][half][:, :, 64], ALU.add),
                [r_nd[0][half], r_nd[1][half]], [r_den[half]]))
            self.defer(lambda d_=d_, half=half: sc.op("dve", lambda e: e.tensor_tensor(
                d_[:, :], d_[:, :], nd[2][half][:, :, 64], ALU.add),
                [r_den[half], r_nd[2][half]], [r_den[half]]))
            self.defer(lambda d_=d_, half=half: sc.op("dve", lambda e: e.reciprocal(d_[:, :], d_[:, :]),
                                                      [r_den[half]], [r_den[half]]))
        for g in range(3):
            for half in range(2):
                self.defer(lambda g=g, half=half: sc.op("dve", lambda e: e.tensor_tensor(
                    ybt[g][:, :, half * 64:(half + 1) * 64], nd[g][half][:, :, 0:64],
                    den[half][:, :].unsqueeze(2).to_broadcast([128, 16, 64]), ALU.mult),
                    [r_nd[g][half], r_den[half]], [r_ybt[g]]))
            yi = cnt["ys"] % 2
            cnt["ys"] += 1
            ys = ystage[yi]
            for bt in range(2):
                bi = cnt["t"] % 3
                cnt["t"] += 1
                tbb = tb[bi]

                def trs(e, g=g, bt=bt, tbb=tbb):
                    ins = None
                    for j in range(8):
                        ins = e.transpose(tbb[:, j * 128:(j + 1) * 128], ybt[g][:, bt * 8 + j, :], self.ident[:, :])
                    return ins
                self.defer(lambda trs=trs, g=g, bi=bi: sc.op("pe", trs, [r_ybt[g]], [r_tb[bi]]))
                self.defer(lambda ys=ys, bt=bt, tbb=tbb, bi=bi, yi=yi: sc.op(
                    "dve", lambda e: e.tensor_copy(ys[:, bt * 1024:(bt + 1) * 1024], tbb[:, :]),
                    [r_tb[bi]], [r_ys[yi]]))
            ch = 6 + g * 2 + ip
            self.defer(lambda ch=ch, ys=ys, yi=yi: self.dma("sp", self.yT[ch * 128:(ch + 1) * 128, :], ys[:, :],
                                                             [r_ys[yi]], []))

    def phase_d(self, l, s):
        sc, ar = self.sc, self.ar
        self.phase_begin("D")
        PADW = 8 + S + 8
        cp = [ar.alloc([128, PADW], F32, "cp") for _ in range(2)]
        r_cp = [Res(), Res()]
        inv = [ar.alloc([128, S], F32, "inv") for _ in range(2)]
        r_inv = [Res(), Res()]
        acc = [ar.alloc([128, S], F32, "acc") for _ in range(2)]
        r_ac = [Res(), Res()]
        dpb = [ar.alloc([128, S], BF16, "dpb") for _ in range(2)]
        r_dp = [Res(), Res()]
        pw = [ar.alloc([128, 128], BF16, "pw") for _ in range(2)]
        r_pw = [Res(), Res()]
        psc = ar.alloc([128, 4], F32, "psc")
        r_psc = Res()
        for g in range(4):
            self.dma("sp", psc[:, g:g + 1], self.pool_scale[l, g * 128:(g + 1) * 128].rearrange("(e o) -> e o", o=1),
                     writes=[r_psc])
        ys = [ar.alloc([128, S], BF16, "ys") for _ in range(2)]
        r_ys = [Res(), Res()]
        bks = self.banks[0:4]
        r_bk = [Res() for _ in range(4)]
        for k in range(2):
            sc.op("dve", lambda e, k=k: e.memset(cp[k][:, 0:8], 0.0), [], [r_cp[k]])
            sc.op("dve", lambda e, k=k: e.memset(cp[k][:, 8 + S:PADW], 0.0), [], [r_cp[k]])
        nb = 0
        for g, win in enumerate((2, 4, 8, 16)):
            k = g % 2
            rad = win // 2
            eng = "dve"
            self.dma("sp", cp[k][:, 8:8 + S], self.cscr[g * 128:(g + 1) * 128, :], writes=[r_cp[k]])
            self.dma("sp", inv[k][:, :], self.c_invcnt[g:g + 1, :].partition_broadcast(128), writes=[r_inv[k]])
            self.dma("pool", pw[k][:, :], self.pool_w[l, g], writes=[r_pw[k]])
            a = acc[k]
            offs = [d_ for d_ in range(-rad, rad + 1)]
            sc.op(eng, lambda e, a=a, k=k, o0=offs[0], o1=offs[1]: e.tensor_tensor(
                a[:, :], cp[k][:, 8 + o0:8 + o0 + S], cp[k][:, 8 + o1:8 + o1 + S], ALU.add), [r_cp[k]], [r_ac[k]])
            for o_ in offs[2:]:
                sc.op(eng, lambda e, a=a, k=k, o_=o_: e.tensor_tensor(
                    a[:, :], a[:, :], cp[k][:, 8 + o_:8 + o_ + S], ALU.add), [r_cp[k], r_ac[k]], [r_ac[k]])
            sc.op(eng, lambda e, a=a, k=k: e.tensor_tensor(a[:, :], a[:, :], inv[k][:, :], ALU.mult),
                  [r_ac[k], r_inv[k]], [r_ac[k]])
            sc.op(eng, lambda e, a=a, k=k: e.tensor_tensor(dpb[k][:, :], a[:, :], cp[k][:, 8:8 + S], ALU.subtract),
                  [r_ac[k], r_cp[k]], [r_dp[k]])
            for tc in range(4):
                bi = nb % 4
                nb += 1
                b = bks[bi]
                sc.op("pe", lambda e, b=b, k=k, tc=tc: e.matmul(b[:, :], pw[k][:, :], dpb[k][:, tc * 512:(tc + 1) * 512],
                                                              start=True, stop=True), [r_pw[k], r_dp[k]], [r_bk[bi]])
                sc.op("act", lambda e, b=b, k=k, tc=tc, g=g: e.activation(
                    ys[k][:, tc * 512:(tc + 1) * 512], b[:, :], AF.Copy, scale=psc[:, g:g + 1]),
                    [r_bk[bi], r_psc], [r_ys[k]])
            self.dma("sp", self.yT[(12 + g) * 128:(13 + g) * 128, :], ys[k][:, :], [r_ys[k]], [])

    def phase_e1(self, l, s):
        sc, ar = self.sc, self.ar
        self.phase_begin("E1")
        yT = ar.alloc([128, 16, S], BF16, "yT")
        r_y = Res()
        for ch in range(16):
            self.dma("sp", yT[:, ch, :], self.yT[ch * 128:(ch + 1) * 128, :], writes=[r_y])
        wp = [ar.alloc([128, 16, 512], BF16, "wp") for _ in range(2)]
        r_wp = [Res(), Res()]
        G = [ar.alloc([128, 3, S], BF16, "G") for _ in range(2)]
        r_G = [Res(), Res()]
        tt_ = [[ar.alloc([128, 512], F32, "tt") for _ in range(3)] for _ in range(2)]
        r_tt = [[Res() for _ in range(3)] for _ in range(2)]
        ms = [ar.alloc([128, S], BF16, "ms") for _ in range(2)]
        r_ms = [Res(), Res()]
        bks = self.banks
        r_bk = [Res() for _ in range(8)]
        nb = 0
        nt = 0
        wa = self.w_proj_a[l].rearrange("(kc p) n -> p kc n", p=128)
        wb_ = self.w_proj_b[l].rearrange("(kc p) n -> p kc n", p=128)
        wc = self.w_proj_c[l].rearrange("(kc p) n -> p kc n", p=128)

        def load_g(dmc):
            for br in range(3):
                self.dma("sp", G[dmc % 2][:, br, :],
                         self.gscr[br * 2048 + dmc * 128:br * 2048 + (dmc + 1) * 128, :], writes=[r_G[dmc % 2]])

        for nbk in range(4):
            w = wp[nbk % 2]
            rw = r_wp[nbk % 2]
            cs = slice(nbk * 512, (nbk + 1) * 512)
            self.dma("pool", w[:, 0:6, :], wa[:, :, cs], writes=[rw])
            self.dma("pool", w[:, 6:12, :], wb_[:, :, cs], writes=[rw])
            self.dma("pool", w[:, 12:16, :], wc[:, :, cs], writes=[rw])
            for dl in range(4):
                dmc = nbk * 4 + dl
                gk = dmc % 2
                if dmc == 0:
                    load_g(0)
                if dmc + 1 < 16:
                    load_g(dmc + 1)
                mst = ms[dmc % 2]
                rms_ = r_ms[dmc % 2]
                for tc in range(4):
                    ts = slice(tc * 512, (tc + 1) * 512)
                    tk = nt % 2
                    nt += 1
                    for br, (k0, k1) in enumerate(((0, 6), (6, 12), (12, 16))):
                        bi = nb % 8
                        nb += 1
                        b = bks[bi]

                        def mm(e, b=b, w=w, dl=dl, ts=ts, k0=k0, k1=k1):
                            ins = None
                            for kc in range(k0, k1):
                                ins = e.matmul(b[:, :], w[:, kc, dl * 128:(dl + 1) * 128], yT[:, kc, ts],
                                               start=(kc == k0), stop=(kc == k1 - 1))
                            return ins
                        sc.op("pe", mm, [rw, r_y], [r_bk[bi]])
                        t = tt_[tk][br]
                        sc.op("dve", lambda e, t=t, b=b, gk=gk, br=br, ts=ts: e.tensor_tensor(
                            t[:, :], b[:, :], G[gk][:, br, ts], ALU.mult), [r_bk[bi], r_G[gk]], [r_tt[tk][br]])
                    t0, t1, t2 = tt_[tk]
                    sc.op("dve", lambda e, t0=t0, t1=t1: e.tensor_tensor(t0[:, :], t0[:, :], t1[:, :], ALU.add),
                          [r_tt[tk][0], r_tt[tk][1]], [r_tt[tk][0]])
                    sc.op("dve", lambda e, t0=t0, t2=t2, mst=mst, ts=ts: e.tensor_tensor(
                        mst[:, ts], t0[:, :], t2[:, :], ALU.add), [r_tt[tk][0], r_tt[tk][2]], [rms_])
                self.dma("sp", self.mT[dmc * 128:(dmc + 1) * 128, :], mst[:, :], [rms_], [])

    def proj_residual(self, aT_src, nkc, wsrc, xin, xout):
        sc, ar = self.sc, self.ar
        aT = ar.alloc([128, nkc, S], BF16, "aT")
        r_a = Res()
        for ch in range(nkc):
            self.dma("sp", aT[:, ch, :], aT_src[ch * 128:(ch + 1) * 128, :], writes=[r_a])
        wo = [ar.alloc([128, nkc, 512], BF16, "wo") for _ in range(2)]
        r_wo = [Res(), Res()]
        xt = [ar.alloc([128, 512], F32, "xt") for _ in range(3)]
        r_xt = [Res() for _ in range(3)]
        xo = [ar.alloc([128, 512], F32, "xo") for _ in range(3)]
        r_xo = [Res() for _ in range(3)]
        bks = self.banks
        r_bk = [Res() for _ in range(8)]
        wv = wsrc.rearrange("(kc p) n -> p kc n", p=128)
        n = 0

        def load_x(m):
            if m >= 64:
                return
            nb_, t_ = m // 16, m % 16
            self.dma("sp", xt[m % 3][:, :], xin[t_ * 128:(t_ + 1) * 128, nb_ * 512:(nb_ + 1) * 512],
                     writes=[r_xt[m % 3]])

        load_x(0)
        for nbk in range(4):
            w = wo[nbk % 2]
            rw = r_wo[nbk % 2]
            cs = slice(nbk * 512, (nbk + 1) * 512)
            self.dma("pool", w[:, :, :], wv[:, :, cs], writes=[rw])
            for tt in range(16):
                rs_ = slice(tt * 128, (tt + 1) * 128)
                i3 = n % 3
                bi = n % 8
                load_x(n + 1)
                n += 1
                b = bks[bi]

                def mm(e, b=b, w=w, rs_=rs_):
                    ins = None
                    for kc in range(nkc):
                        ins = e.matmul(b[:, :], aT[:, kc, rs_], w[:, kc, :], start=(kc == 0), stop=(kc == nkc - 1))
                    return ins
                sc.op("pe", mm, [r_a, rw], [r_bk[bi]])
                sc.op("dve", lambda e, b=b, i3=i3: e.tensor_tensor(xo[i3][:, :], b[:, :], xt[i3][:, :], ALU.add),
                      [r_bk[bi], r_xt[i3]], [r_xo[i3]])
                self.dma("sp", xout[rs_, cs], xo[i3][:, :], [r_xo[i3]], [])

    def phase_e2(self, l, s, xin, xout):
        self.phase_begin("E2")
        self.proj_residual(self.mT, 16, self.w_out[l], xin, xout)

    def phase_f1(self, l, s, xsrc):
        sc, ar = self.sc, self.ar
        self.phase_begin("F1")
        hT = ar.alloc([128, 16, S], BF16, "hT")
        r_hT = Res()
        self.norm_to_hT(xsrc, self.norm2_g[l:l + 1, :], hT, r_hT)
        wb = [ar.alloc([128, 16, 512], BF16, "wb") for _ in range(2)]
        r_wb = [Res(), Res()]
        rl = [ar.alloc([128, 512], F32, "rl") for _ in range(3)]
        r_rl = [Res() for _ in range(3)]
        ust = [ar.alloc([128, S], BF16, "ust") for _ in range(3)]
        r_us = [Res() for _ in range(3)]
        bks = self.banks[2:8]
        r_bk = [Res() for _ in range(6)]
        wv = self.w_up[l].rearrange("(kc p) n -> p kc n", p=128)
        n = 0
        for jb in range(DFF // 512):
            w = wb[jb % 2]
            rw = r_wb[jb % 2]
            self.dma("pool", w[:, :, :], wv[:, :, jb * 512:(jb + 1) * 512], writes=[rw])
            for lc in range(4):
                ffc = jb * 4 + lc
                us = ust[ffc % 3]
                rus = r_us[ffc % 3]
                for tc in range(4):
                    bi = n % 6
                    ri = n % 3
                    n += 1
                    b = bks[bi]

                    def mm(e, b=b, w=w, lc=lc, tc=tc):
                        ins = None
                        for kc in range(16):
                            ins = e.matmul(b[:, :], w[:, kc, lc * 128:(lc + 1) * 128], hT[:, kc, tc * 512:(tc + 1) * 512],
                                           start=(kc == 0), stop=(kc == 15))
                        return ins
                    sc.op("pe", mm, [r_hT, rw], [r_bk[bi]])
                    r_ = rl[ri]
                    sc.op("act", lambda e, r_=r_, b=b: e.activation(r_[:, :], b[:, :], AF.Relu), [r_bk[bi]], [r_rl[ri]])
                    sc.op("dve", lambda e, r_=r_, us=us, tc=tc: e.tensor_tensor(
                        us[:, tc * 512:(tc + 1) * 512], r_[:, :], r_[:, :], ALU.mult), [r_rl[ri]], [rus])
                dst = bass.AP(self.us.tensor, ffc * 128, [[64 * 128, 128], [128 * 64 * 128, 16], [1, 128]])
                self.dma("sp", dst, us[:, :].rearrange("p (t q) -> p t q", t=16), [rus], [])

    def phase_f2(self, l, s, xin, xout):
        sc, ar = self.sc, self.ar
        self.phase_begin("F2")
        NQ = 4
        wd = [ar.alloc([128, 16, 512], BF16, "wd") for _ in range(8)]
        r_wd = [Res() for _ in range(8)]
        ut = [ar.alloc([128, 64, 128], BF16, "ut") for _ in range(2)]
        r_ut = [Res(), Res()]

        def load_ux(m):
            if m >= 64:
                return
            nb_, t_ = m // 16, m % 16
            self.dma("sp", ut[m % 2][:, :, :], self.us[t_], writes=[r_ut[m % 2]])
            self.dma("sp", xt[m % 3][:, :], xin[t_ * 128:(t_ + 1) * 128, nb_ * 512:(nb_ + 1) * 512],
                     writes=[r_xt[m % 3]])

        xt = [ar.alloc([128, 512], F32, "xt") for _ in range(3)]
        r_xt = [Res() for _ in range(3)]
        xo = [ar.alloc([128, 512], F32, "xo") for _ in range(3)]
        r_xo = [Res() for _ in range(3)]
        bks = self.banks
        r_bk = [Res() for _ in range(8)]
        wv = self.w_down[l].rearrange("(kc p) n -> p kc n", p=128)
        n = 0
        for nbk in range(4):
            cs = slice(nbk * 512, (nbk + 1) * 512)
            wq = []
            for q in range(NQ):
                wi = (nbk * NQ + q) % 8
                self.dma("pool", wd[wi][:, :, :], wv[:, q * 16:(q + 1) * 16, cs], writes=[r_wd[wi]])
                wq.append((wd[wi], r_wd[wi]))
            for tt in range(16):
                rs_ = slice(tt * 128, (tt + 1) * 128)
                i3 = n % 3
                bi = n % 8
                u = ut[n % 2]
                ru = r_ut[n % 2]
                if n == 0:
                    load_ux(0)
                load_ux(n + 1)
                n += 1
                b = bks[bi]

                def mm(e, b=b, wq=wq, u=u):
                    ins = None
                    for kc in range(64):
                        w = wq[kc // 16][0]
                        ins = e.matmul(b[:, :], u[:, kc, :], w[:, kc % 16, :], start=(kc == 0), stop=(kc == 63))
                    return ins
                sc.op("pe", mm, [ru] + [q_[1] for q_ in wq], [r_bk[bi]])
                sc.op("dve", lambda e, b=b, i3=i3: e.tensor_tensor(xo[i3][:, :], b[:, :], xt[i3][:, :], ALU.add),
                      [r_bk[bi], r_xt[i3]], [r_xo[i3]])
                self.dma("sp", xout[rs_, cs], xo[i3][:, :], [r_xo[i3]], [])

    def phase_final(self, s, xsrc, dst):
        sc, ar = self.sc, self.ar
        self.phase_begin("FIN")
        gbc = ar.alloc([128, D], F32, "gbc")
        r_g = Res()
        self.dma("sp", gbc[:, :], self.final_g.partition_broadcast(128), writes=[r_g])
        xt = [ar.alloc([128, D], F32, "xt") for _ in range(3)]
        r_xt = [Res() for _ in range(3)]
        junk = ar.alloc([128, D], BF16, "junk")
        r_junk = Res()
        ss = [ar.alloc([128, 1], F32, "ss") for _ in range(3)]
        r_ss = [Res() for _ in range(3)]
        xo = [ar.alloc([128, D], F32, "xo") for _ in range(3)]
        r_xo = [Res() for _ in range(3)]
        for tt in range(S // 128):
            k = tt % 3
            rs_ = slice(tt * 128, (tt + 1) * 128)
            self.dma("sp", xt[k][:, :], xsrc[rs_, :], writes=[r_xt[k]])
            sc.op("act", lambda e, k=k: e.activation(junk[:, :], xt[k][:, :], AF.Square, scale=1.0 / math.sqrt(D),
                                                     accum_out=ss[k][:, :]), [r_xt[k]], [r_junk, r_ss[k]])
            sc.op("act", lambda e, k=k: e.activation(ss[k][:, :], ss[k][:, :], AF.Sqrt, bias=EPS), [r_ss[k]], [r_ss[k]])
            sc.op("dve", lambda e, k=k: e.reciprocal(ss[k][:, :], ss[k][:, :]), [r_ss[k]], [r_ss[k]])
            sc.op("dve", lambda e, k=k: e.scalar_tensor_tensor(xo[k][:, :], xt[k][:, :], ss[k][:, 0:1], gbc[:, :],
                                                               ALU.mult, ALU.mult), [r_xt[k], r_ss[k], r_g], [r_xo[k]])
            self.dma("sp", dst[rs_, :], xo[k][:, :], [r_xo[k]], [])

    def build(self):
        nc, sc = self.nc, self.sc
        self.setup()
        done = False
        for l in range(self.n_layers):
            for s in range(self.n_seq):
                sl = slice(s * S, (s + 1) * S)
                xprev = self.x if l == 0 else self.xb
                stages = [("a", lambda: self.phase_a(l, s, xprev[sl, :])),
                          ("b", lambda: self.phase_b(l, s)),
                          ("c", lambda: self.phase_c(l, s)),
                          ("d", lambda: self.phase_d(l, s)),
                          ("e1", lambda: self.phase_e1(l, s)),
                          ("e2", lambda: self.phase_e2(l, s, xprev[sl, :], self.xa[sl, :])),
                          ("f1", lambda: self.phase_f1(l, s, self.xa[sl, :])),
                          ("f2", lambda: self.phase_f2(l, s, self.xa[sl, :], self.xb[sl, :]))]
                for name, fn in stages:
                    fn()
                    if self.stop_after == name:
                        done = True
                        break
                if done:
                    break
            if done:
                break
        if not done:
            for s in range(self.n_seq):
                sl = slice(s * S, (s + 1) * S)
                self.phase_final(s, self.xb[sl, :], self.out[sl, :])
        fin = sc.finish()
        with ExitStack() as st:
            engsem = {e: st.enter_context(nc.semaphore(f"es_{e}")) for e in ENG}
            dmasems = {e: [st.enter_context(nc.semaphore(f"ds_{e}_{i}")) for i in range(NDMASEM)]
                       for e in ("sp", "pool")}
            block = st.enter_context(nc.Block())
            sc.emit(nc, block, engsem, dmasems)
        return nc


def t5_bucket_np(rel):
    n = np.abs(rel)
    sign_off = np.where(rel > 0, 16, 0)
    thr = np.array([15, 27, 50, 91, 166, 305, 559])
    large = 8 + (n[..., None] >= thr).sum(-1)
    return sign_off + np.where(n < 8, n, large)


def make_consts():
    j = np.arange(GW)
    rel = U0 + 127 - j
    bucket = t5_bucket_np(rel)
    onehot = np.zeros((32, GW), np.float32)
    onehot[bucket, j] = 1.0
    mask = np.zeros((18, GW), np.float32)
    mask[0:6] = 1.0
    for g, (win, r) in enumerate(((128, 1), (512, 4), (2048, 16))):
        ok = ((rel % r) == 0) & (np.abs(rel) <= (win // (2 * r)) * r)
        mask[6 + 4 * g:10 + 4 * g] = ok.astype(np.float32)[None, :]
    mask[:, GW - 1] = 0.0
    ident = np.eye(128, dtype=np.float32)
    antiid = np.ascontiguousarray(ident[::-1])
    pos = np.arange(S)
    invcnt = np.zeros((4, S), np.float32)
    for g, win in enumerate((2, 4, 8, 16)):
        rad = win // 2
        lo = np.clip(pos - rad, 0, S)
        hi = np.clip(pos + rad + 1, 0, S)
        invcnt[g] = 1.0 / (hi - lo).astype(np.float32)
    return {"c_onehot": onehot, "c_mask": mask, "c_ident": ident, "c_antiid": antiid, "c_invcnt": invcnt}


_PROG_CACHE = {}


def kernel(**inputs):
    n = 8
    x = np.ascontiguousarray(np.asarray(inputs["x"], dtype=np.float32))
    B = x.shape[0]
    per = B // n
    consts = make_consts()
    shared = {}
    for k, v in inputs.items():
        if k == "x":
            continue
        a = np.ascontiguousarray(np.asarray(v, dtype=np.float32))
        if k == "final_g":
            a = a.reshape(1, D)
        shared[k] = a
    shared.update(consts)
    if "prog" not in _PROG_CACHE:
        _PROG_CACHE["prog"] = Prog().build()
    nc = _PROG_CACHE["prog"]
    in_maps = []
    for c in range(n):
        m = dict(shared)
        m["x"] = x[c * per:(c + 1) * per].reshape(per * S, D)
        in_maps.append(m)
    res = run_bass_kernel_spmd(nc, in_maps, core_ids=list(range(n)))
    outs = [r["out"].reshape(per, S, D) for r in res.results]
    return np.concatenate(outs, axis=0).astype(np.float32)
```

```python
import math
import os
from contextlib import ExitStack

import numpy as np
import concourse.bass as bass
import concourse.mybir as mybir
from concourse.bass_utils import run_bass_kernel_spmd

F32 = mybir.dt.float32
BF16 = mybir.dt.bfloat16
AF = mybir.ActivationFunctionType
ALU = mybir.AluOpType
AX = mybir.AxisListType

D = 2048
S = 2048
DEPTH = 4
NSEQ = 2
IN_COLS = 11264
DFF = 8192
EPS = 1e-6
U0 = 1920
RW = 3968
GW = 4096
ENG = ("pe", "act", "dve", "pool", "sp")
NDMASEM = 20


class Res:
    __slots__ = ("name", "w", "r", "rd")

    def __init__(self, name=""):
        self.name = name
        self.w = None
        self.r = {}
        self.rd = []


class Op:
    __slots__ = ("eng", "fn", "deps", "need_inc", "idx", "dma", "sem", "semval", "semprev", "phase")

    def __init__(self, eng, fn, dma):
        self.eng = eng
        self.fn = fn
        self.dma = dma
        self.deps = ()
        self.need_inc = False
        self.idx = 0
        self.sem = None
        self.semval = 0
        self.semprev = 0


class Sched:
    def __init__(self):
        self.ops = {e: [] for e in ENG}
        self.since_barrier = []
        self.last = {e: None for e in ENG}
        self.barrier_deps = {e: [] for e in ENG}
        self.phase = "setup"
        self.scopes = False

    def op(self, eng, fn, reads=(), writes=(), dma=False):
        o = Op(eng, fn, dma)
        o.phase = self.phase
        deps = set(self.barrier_deps[eng])
        self.barrier_deps[eng] = []
        for r in reads:
            if r.w is not None:
                deps.add(r.w)
        for w in writes:
            if w.w is not None:
                deps.add(w.w)
            deps.update(w.r.values())
            deps.update(w.rd)
        dl = []
        for d in deps:
            if d is o:
                continue
            if (not d.dma) and (not dma) and d.eng == "pe" and eng == "pe":
                continue
            dl.append(d)
            if not d.dma:
                d.need_inc = True
        o.deps = dl
        for r in reads:
            if dma:
                r.rd.append(o)
            else:
                r.r[eng] = o
        for w in writes:
            w.w = o
            w.r = {}
            w.rd = []
        self.ops[eng].append(o)
        self.last[eng] = o
        if dma:
            self.since_barrier.append(o)
        return o

    def barrier(self):
        deps = list(self.since_barrier)
        for e in ENG:
            if self.last[e] is not None:
                deps.append(self.last[e])
        self.since_barrier = []
        for e in ENG:
            self.barrier_deps[e] = list(deps)

    def finish(self):
        self.barrier()
        return self.op("sp", None)

    def emit(self, nc, block, engsem, dmasems):
        for e in ENG:
            c = 0
            nd = 0
            for o in self.ops[e]:
                if o.dma:
                    k = nd % NDMASEM
                    o.sem = dmasems[e][k]
                    o.semprev = 16 * (nd // NDMASEM)
                    o.semval = o.semprev + 16
                    nd += 1
                elif o.need_inc:
                    c += 1
                    o.idx = c
        handles = {"pe": block.tensor, "act": block.scalar, "dve": block.vector,
                   "pool": block.gpsimd, "sp": block.sync}

        def make(e):
            def run(eng):
                seen = {}
                cur = None
                for o in self.ops[e]:
                    if self.scopes and (cur is None or o.phase != cur[0]):
                        if cur is not None:
                            nc.leave_named_scope(cur[0], cur[1], False)
                        sid, _ = nc.enter_named_scope(o.phase, False)
                        cur = (o.phase, sid)
                    if self.scopes:
                        cur_name = cur[0]
                    waits = {}
                    for d in o.deps:
                        if d.dma:
                            key, val = d.sem, d.semval
                        else:
                            key, val = engsem[d.eng], d.idx
                        kid = id(key)
                        if kid not in waits or waits[kid][1] < val:
                            waits[kid] = (key, val)
                    if o.dma and o.semprev > 0:
                        kid = id(o.sem)
                        if kid not in waits or waits[kid][1] < o.semprev:
                            waits[kid] = (o.sem, o.semprev)
                    for kid, (key, val) in waits.items():
                        if seen.get(kid, 0) >= val:
                            continue
                        eng.wait_ge(key, val)
                        seen[kid] = val
                    if o.fn is None:
                        continue
                    ins = o.fn(eng)
                    if o.dma:
                        ins.then_inc(o.sem, 16)
                    elif o.need_inc:
                        ins.then_inc(engsem[e], 1)
                if self.scopes and cur is not None:
                    nc.leave_named_scope(cur[0], cur[1], False)
            return run

        for e in ENG:
            handles[e](make(e))


class Arena:
    def __init__(self, nc, base, limit):
        self.nc = nc
        self.base = base
        self.cur = base
        self.limit = limit
        self.n = 0

    def reset(self):
        self.cur = self.base

    def alloc(self, shape, dtype, name="t"):
        nbytes = int(np.prod(shape[1:])) * (4 if dtype == F32 else 2)
        nbytes = (nbytes + 63) // 64 * 64
        off = self.cur
        self.cur += nbytes
        assert self.cur <= self.limit, f"arena overflow {self.cur} > {self.limit} ({name})"
        self.n += 1
        return self.nc.alloc_sbuf_tensor_at(f"{name}_{self.n}", list(shape), dtype, offset=off)


def rot(lst, i):
    return lst[i % len(lst)]


class Prog:
    def __init__(self, n_layers=DEPTH, n_seq=NSEQ, taps=False, stop_after=None):
        self.n_layers = n_layers
        self.n_seq = n_seq
        self.taps = taps
        self.stop_after = stop_after
        nc = bass.Bass("TRN2", target_bir_lowering=False)
        self.nc = nc
        self.sc = Sched()
        T = NSEQ * S

        def din(name, shape):
            return nc.dram_tensor(name, list(shape), F32, kind="ExternalInput").ap()

        self.x = din("x", [T, D])
        self.table = din("rel_bias_table", [32, 18])
        self.norm1_g = din("norm1_g", [DEPTH, D])
        self.w_in = din("w_in", [DEPTH, D, IN_COLS])
        self.lq1 = din("lambda_q1", [DEPTH, 64])
        self.lk1 = din("lambda_k1", [DEPTH, 64])
        self.lq2 = din("lambda_q2", [DEPTH, 64])
        self.lk2 = din("lambda_k2", [DEPTH, 64])
        self.subln_g = din("subln_g", [DEPTH, 128])
        self.pool_w = din("pool_w", [DEPTH, 4, 128, 128])
        self.pool_scale = din("pool_scale", [DEPTH, 512])
        self.w_proj_a = din("w_proj_a", [DEPTH, 768, D])
        self.w_proj_b = din("w_proj_b", [DEPTH, 768, D])
        self.w_proj_c = din("w_proj_c", [DEPTH, 512, D])
        self.w_out = din("w_out", [DEPTH, D, D])
        self.norm2_g = din("norm2_g", [DEPTH, D])
        self.w_up = din("w_up", [DEPTH, D, DFF])
        self.w_down = din("w_down", [DEPTH, DFF, D])
        self.final_g = din("final_g", [1, D])
        self.c_onehot = din("c_onehot", [32, GW])
        self.c_mask = din("c_mask", [18, GW])
        self.c_ident = din("c_ident", [128, 128])
        self.c_antiid = din("c_antiid", [128, 128])
        self.c_invcnt = din("c_invcnt", [4, S])

        self.out = nc.dram_tensor("out", [T, D], F32, kind="ExternalOutput").ap()

        def scr(name, shape, dt):
            kind = "ExternalOutput" if taps else "Internal"
            return nc.dram_tensor(name, list(shape), dt, kind=kind).ap()

        self.grow = scr("s_grow", [18, GW], BF16)
        self.rscr = scr("s_r", [18, 128, RW], BF16)
        self.qk = scr("s_qk", [24 * 128, S], BF16)
        self.va = scr("s_va", [S, 768], BF16)
        self.vb = scr("s_vb", [S, 768], BF16)
        self.cscr = scr("s_c", [512, S], F32)
        self.gscr = scr("s_g", [6144, S], BF16)
        self.yT = scr("s_yT", [16 * 128, S], BF16)
        self.mT = scr("s_mT", [16 * 128, S], BF16)
        self.xa = scr("s_xa", [T, D], F32)
        self.xb = scr("s_xb", [T, D], F32)
        self.us = scr("s_u", [16, 128, 64, 128], BF16)

        self.ar = Arena(nc, 20480, 229344)
        self.ident = self.ar.alloc([128, 128], BF16, "ident")
        self.antiid = self.ar.alloc([128, 128], BF16, "antiid")
        self.ones = self.ar.alloc([128, 128], BF16, "ones")
        self.ar.base = self.ar.cur
        self.banks = [nc.alloc_psum_tensor(f"bank{i}", [128, 512], F32) for i in range(8)]

    def dma(self, eng, out, in_, reads=(), writes=()):
        return self.sc.op(eng, lambda e: e.dma_start(out=out, in_=in_), reads, writes, dma=True)

    def phase_begin(self, name=None):
        self.sc.barrier()
        self.ar.reset()
        if name is not None:
            self.sc.phase = name

    def setup(self):
        sc, ar = self.sc, self.ar
        self.phase_begin()
        r_const = Res("const")
        identf = ar.alloc([128, 128], F32, "identf")
        tab = ar.alloc([32, 18], F32, "tab")
        oh = ar.alloc([32, GW], F32, "oh")
        msk = ar.alloc([18, GW], F32, "msk")
        gf = ar.alloc([18, GW], F32, "gf")
        gb = ar.alloc([18, GW], BF16, "gb")
        r_in = Res()
        self.dma("pool", self.ident[:], self.c_ident, writes=[r_const])
        self.dma("pool", self.antiid[:], self.c_antiid, writes=[r_const])
        sc.op("dve", lambda e: e.memset(self.ones[:, :], 1.0), [], [r_const])
        self.dma("sp", tab[:], self.table, writes=[r_in])
        self.dma("sp", oh[:], self.c_onehot, writes=[r_in])
        self.dma("sp", msk[:], self.c_mask, writes=[r_in])
        r_gf = Res()
        for i in range(GW // 512):
            b = self.banks[i % 4]
            rb = Res()
            sc.op("pe", lambda e, b=b, i=i: e.matmul(b[0:18, :], tab[:, :], oh[:, i * 512:(i + 1) * 512],
                                                     start=True, stop=True), [r_in], [rb])
            sc.op("act", lambda e, b=b, i=i: e.activation(gf[:, i * 512:(i + 1) * 512], b[0:18, :], AF.Exp),
                  [rb], [r_gf])
        r_gb = Res()
        sc.op("dve", lambda e: e.tensor_tensor(gb[:, :], gf[:, :], msk[:, :], ALU.mult), [r_gf, r_in], [r_gb])
        r_grow = Res()
        self.dma("sp", self.grow, gb[:, :], [r_gb], [r_grow])
        t1 = [ar.alloc([128, RW], BF16, "t1") for _ in range(2)]
        rt = [ar.alloc([128, RW], BF16, "rt") for _ in range(2)]
        r_t1 = [Res(), Res()]
        r_rt = [Res(), Res()]
        r_bank = [Res() for _ in range(8)]
        nb = 0
        for h in range(18):
            src = bass.AP(self.grow.tensor, h * GW, [[1, 128], [1, RW]])
            self.dma("sp", t1[h % 2][:, :], src, [r_grow], [r_t1[h % 2]])
            for i in range((RW + 511) // 512):
                w = min(512, RW - i * 512)
                bi = nb % 8
                nb += 1
                b = self.banks[bi]
                sc.op("pe", lambda e, b=b, i=i, w=w, h=h: e.matmul(
                    b[:, 0:w], self.antiid[:, :], t1[h % 2][:, i * 512:i * 512 + w], start=True, stop=True),
                    [r_t1[h % 2], r_const], [r_bank[bi]])
                eng = "act" if i % 2 == 0 else "dve"
                if eng == "act":
                    sc.op("act", lambda e, b=b, i=i, w=w, h=h: e.copy(rt[h % 2][:, i * 512:i * 512 + w], b[:, 0:w]),
                          [r_bank[bi]], [r_rt[h % 2]])
                else:
                    sc.op("dve", lambda e, b=b, i=i, w=w, h=h: e.tensor_copy(rt[h % 2][:, i * 512:i * 512 + w], b[:, 0:w]),
                          [r_bank[bi]], [r_rt[h % 2]])
            self.dma("sp", self.rscr[h], rt[h % 2][:, :], [r_rt[h % 2]], [])

    def norm_to_hT(self, xsrc, grow_ap, hT, r_hT):
        sc, ar = self.sc, self.ar
        gbc = ar.alloc([128, D], F32, "gbc")
        r_g = Res()
        self.dma("sp", gbc[:, :], grow_ap.partition_broadcast(128), writes=[r_g])
        xt = [ar.alloc([128, D], F32, "xt") for _ in range(2)]
        r_xt = [Res(), Res()]
        junk = ar.alloc([128, D], BF16, "junk")
        r_junk = Res()
        ss = [ar.alloc([128, 1], F32, "ss") for _ in range(2)]
        r_ss = [Res(), Res()]
        xn = [ar.alloc([128, D], BF16, "xn") for _ in range(2)]
        r_xn = [Res(), Res()]
        r_b = [Res(), Res()]
        for tt in range(S // 128):
            k = tt % 2
            self.dma("sp", xt[k][:, :], xsrc[tt * 128:(tt + 1) * 128, :], writes=[r_xt[k]])
            sc.op("act", lambda e, k=k: e.activation(junk[:, :], xt[k][:, :], AF.Square, scale=1.0 / math.sqrt(D),
                                                     accum_out=ss[k][:, :]),
                  [r_xt[k]], [r_junk, r_ss[k]])
            sc.op("act", lambda e, k=k: e.activation(ss[k][:, :], ss[k][:, :], AF.Sqrt, bias=EPS),
                  [r_ss[k]], [r_ss[k]])
            sc.op("dve", lambda e, k=k: e.reciprocal(ss[k][:, :], ss[k][:, :]),
                  [r_ss[k]], [r_ss[k]])
            sc.op("dve", lambda e, k=k: e.scalar_tensor_tensor(xn[k][:, :], xt[k][:, :], ss[k][:, 0:1], gbc[:, :],
                                                               ALU.mult, ALU.mult),
                  [r_xt[k], r_ss[k], r_g], [r_xn[k]])
            for hf in range(2):
                pb = self.banks[hf].ap().bitcast(BF16)

                def tr(e, k=k, hf=hf, pb=pb):
                    ins = None
                    for j in range(8):
                        kc = hf * 8 + j
                        ins = e.transpose(pb[:, j * 128:(j + 1) * 128], xn[k][:, kc * 128:(kc + 1) * 128],
                                          self.ident[:, :])
                    return ins
                sc.op("pe", tr, [r_xn[k]], [r_b[hf]])
                dst = hT[:, hf * 8:(hf + 1) * 8, tt * 128:(tt + 1) * 128]
                srcp = pb.rearrange("p (j q) -> p j q", j=8)
                if hf == 0:
                    sc.op("act", lambda e, dst=dst, srcp=srcp: e.copy(dst, srcp), [r_b[hf]], [r_hT])
                else:
                    sc.op("dve", lambda e, dst=dst, srcp=srcp: e.tensor_copy(dst, srcp), [r_b[hf]], [r_hT])

    def phase_a(self, l, s, xsrc):
        sc, ar = self.sc, self.ar
        self.phase_begin("A")
        hT = ar.alloc([128, 16, S], BF16, "hT")
        r_hT = Res()
        self.norm_to_hT(xsrc, self.norm1_g[l:l + 1, :], hT, r_hT)
        wb = [ar.alloc([128, 16, 512], BF16, "wb") for _ in range(2)]
        r_wb = [Res(), Res()]
        stg = [ar.alloc([128, S], BF16, "stg") for _ in range(3)]
        r_stg = [Res() for _ in range(3)]
        stgv = [ar.alloc([128, 512], BF16, "stgv") for _ in range(3)]
        r_stgv = [Res() for _ in range(3)]
        stgc = [ar.alloc([128, S], F32, "stgc") for _ in range(2)]
        r_stgc = [Res() for _ in range(2)]
        bks = self.banks[2:8]
        r_bk = [Res() for _ in range(6)]
        nbk = 0
        nstg = 0
        nstgv = 0
        nstgc = 0
        nev = 0
        wv = self.w_in[l].rearrange("(kc p) n -> p kc n", p=128)
        for j in range(IN_COLS // 512):
            w = wb[j % 2]
            rw = r_wb[j % 2]
            self.dma("pool", w[:, :, :], wv[:, :, j * 512:(j + 1) * 512], writes=[rw])
            lc = 0
            while lc < 4:
                cc = 4 * j + lc
                zform = (12 <= cc < 18) or (30 <= cc < 36)
                if zform:
                    n = 1
                    while lc + n < 4 and ((12 <= cc + n < 18) or (30 <= cc + n < 36)):
                        n += 1
                    ncol = n * 128
                    if cc < 18:
                        vdst, vc0 = self.va, (cc - 12) * 128
                    else:
                        vdst, vc0 = self.vb, (cc - 30) * 128
                    for tt in range(16):
                        bi = nbk % 6
                        nbk += 1
                        b = bks[bi]

                        def mm(e, b=b, w=w, tt=tt, lc=lc, ncol=ncol):
                            ins = None
                            for kc in range(16):
                                ins = e.matmul(b[:, 0:ncol], hT[:, kc, tt * 128:(tt + 1) * 128],
                                               w[:, kc, lc * 128:lc * 128 + ncol], start=(kc == 0), stop=(kc == 15))
                            return ins
                        sc.op("pe", mm, [r_hT, rw], [r_bk[bi]])
                        si = nstgv % 3
                        nstgv += 1
                        st = stgv[si]
                        if nev % 2 == 0:
                            sc.op("act", lambda e, st=st, b=b, ncol=ncol: e.copy(st[:, 0:ncol], b[:, 0:ncol]),
                                  [r_bk[bi]], [r_stgv[si]])
                        else:
                            sc.op("dve", lambda e, st=st, b=b, ncol=ncol: e.tensor_copy(st[:, 0:ncol], b[:, 0:ncol]),
                                  [r_bk[bi]], [r_stgv[si]])
                        nev += 1
                        self.dma("sp", vdst[tt * 128:(tt + 1) * 128, vc0:vc0 + ncol], st[:, 0:ncol],
                                 [r_stgv[si]], [])
                    lc += n
                    continue
                if cc < 12:
                    kind, dst = "qk", self.qk[cc * 128:(cc + 1) * 128, :]
                elif cc < 30:
                    kind, dst = "qk", self.qk[(cc - 6) * 128:(cc - 5) * 128, :]
                elif cc < 40:
                    kind, dst = "c", self.cscr[(cc - 36) * 128:(cc - 35) * 128, :]
                else:
                    kind, dst = "g", self.gscr[(cc - 40) * 128:(cc - 39) * 128, :]
                if kind == "c":
                    si = nstgc % 2
                    nstgc += 1
                    st, rs = stgc[si], r_stgc[si]
                else:
                    si = nstg % 3
                    nstg += 1
                    st, rs = stg[si], r_stg[si]
                for tc in range(4):
                    bi = nbk % 6
                    nbk += 1
                    b = bks[bi]

                    def mm(e, b=b, w=w, tc=tc, lc=lc):
                        ins = None
                        for kc in range(16):
                            ins = e.matmul(b[:, :], w[:, kc, lc * 128:(lc + 1) * 128],
                                           hT[:, kc, tc * 512:(tc + 1) * 512], start=(kc == 0), stop=(kc == 15))
                        return ins
                    sc.op("pe", mm, [r_hT, rw], [r_bk[bi]])
                    o = st[:, tc * 512:(tc + 1) * 512]
                    if kind == "g":
                        sc.op("act", lambda e, o=o, b=b: e.activation(o, b[:, :], AF.Sigmoid), [r_bk[bi]], [rs])
                    elif nev % 2 == 0:
                        sc.op("act", lambda e, o=o, b=b: e.copy(o, b[:, :]), [r_bk[bi]], [rs])
                        nev += 1
                    else:
                        sc.op("dve", lambda e, o=o, b=b: e.tensor_copy(o, b[:, :]), [r_bk[bi]], [rs])
                        nev += 1
                self.dma("sp", dst, st[:, :], [rs], [])
                lc += 1


    def attn_run(self, steps, nst=4, la=3):
        sc = self.sc
        stb = self.banks[0:nst]
        r_st = [Res() for _ in range(nst)]
        Et = [self.ar.alloc([128, 512], BF16, "E") for _ in range(nst)]
        Pt = [self.ar.alloc([128, 512], BF16, "P") for _ in range(nst)]
        r_E = [Res() for _ in range(nst)]
        r_P = [Res() for _ in range(nst)]
        self.pending = []
        self.cur_gid = 0

        def flush(maxgid=None, count=None):
            n = 0
            while self.pending:
                if maxgid is not None and self.pending[0][0] > maxgid:
                    break
                if count is not None and n >= count:
                    break
                _, fn = self.pending.pop(0)
                fn()
                n += 1

        def emit_st(st, i):
            b = stb[i % nst]
            hs = slice(0, 128)
            kb, qc = st["kb"], st["qc"]
            qT, kT, R = st["qT"], st["kT"], st["R"]
            sc.op("pe", lambda e: e.matmul(b[:, :], kT[hs, kb * 128:(kb + 1) * 128], qT[hs, qc * 512:(qc + 1) * 512],
                                           start=True, stop=True), st["r_in"], [r_st[i % nst]])
            E = Et[i % nst]
            sc.op("act", lambda e: e.activation(E[:, :], b[:, :], AF.Exp, scale=0.125), [r_st[i % nst]], [r_E[i % nst]])
            P = Pt[i % nst]
            s0 = U0 - (kb * 128 - qc * 512)
            sc.op("dve", lambda e: e.tensor_tensor(P[:, :], E[:, :], R[:, s0:s0 + 512], ALU.mult),
                  [r_E[i % nst]] + st["r_in"], [r_P[i % nst]])
            flush(count=1)

        def emit_av(st, i):
            P = Pt[i % nst]
            v = st["V"](st["kb"])
            num_ap, den_ap = st["acc"]
            first, last = st["first"], st["last"]
            if first:
                flush(maxgid=st["gid"] - 2)

            def f(e):
                e.matmul(num_ap, v, P[:, :], start=first, stop=last)
                return e.matmul(den_ap, self.ones[:, :], P[:, :], start=first, stop=last)
            sc.op("pe", f, [r_P[i % nst]] + st["r_in"], [st["r_acc"]])
            if last and st["fin"] is not None:
                self.cur_gid = st["gid"]
                st["fin"]()
            if st.get("post") is not None:
                st["post"]()

        n = len(steps)
        for i in range(n + la):
            if i < n:
                emit_st(steps[i], i)
            if i >= la:
                emit_av(steps[i - la], i - la)
        flush()

    def defer(self, fn):
        self.pending.append((self.cur_gid, fn))

    def phase_b(self, l, s):
        sc, ar = self.sc, self.ar
        self.phase_begin("B")
        lam_init = 0.8 - 0.6 * math.exp(-0.3 * l)
        lv = [ar.alloc([128, 64], F32, "lv") for _ in range(4)]
        r_lv = Res()
        for t, src in zip(lv, (self.lq1, self.lk1, self.lq2, self.lk2)):
            self.dma("sp", t[:, :], src[l:l + 1, :].partition_broadcast(128), writes=[r_lv])
        sm = ar.alloc([128, 8], F32, "sm")
        r_sm = Res()
        pr = ar.alloc([128, 64], F32, "pr")
        r_pr = Res()
        for i in range(2):
            sc.op("dve", lambda e, i=i: e.tensor_tensor(pr[:, :], lv[2 * i][:, :], lv[2 * i + 1][:, :], ALU.mult),
                  [r_lv], [r_pr])
            sc.op("dve", lambda e, i=i: e.reduce_sum(sm[:, i:i + 1], pr[:, :], AX.X), [r_pr], [r_sm])
        sc.op("act", lambda e: e.activation(sm[:, 2:4], sm[:, 0:2], AF.Exp), [r_sm], [r_sm])
        sc.op("dve", lambda e: e.tensor_tensor(sm[:, 4:5], sm[:, 3:4], sm[:, 2:3], ALU.subtract), [r_sm], [r_sm])
        sc.op("dve", lambda e: e.tensor_scalar_add(sm[:, 5:6], sm[:, 4:5], -lam_init), [r_sm], [r_sm])
        neglam = sm[:, 5:6]
        gsub = ar.alloc([128, 1], F32, "gsub")
        r_gs = Res()
        self.dma("sp", gsub[:, :], self.subln_g[l].rearrange("(e o) -> e o", o=1), writes=[r_gs])
        sc.op("dve", lambda e: e.tensor_scalar_mul(gsub[:, :], gsub[:, :], 1.0 - lam_init), [r_gs], [r_gs])
        qT = [[ar.alloc([128, S], BF16, "qT") for _ in range(2)] for _ in range(2)]
        kT = [ar.alloc([128, S], BF16, "kT") for _ in range(2)]
        V = [ar.alloc([128, 16, 128], BF16, "V") for _ in range(2)]
        R = [ar.alloc([128, RW], BF16, "R") for _ in range(2)]
        r_hd = [Res(), Res()]
        for k in range(2):
            sc.op("dve", lambda e, k=k: e.memset(qT[k][0][64:128, :], 0.0), [], [r_hd[k]])
            sc.op("dve", lambda e, k=k: e.memset(qT[k][1][0:64, :], 0.0), [], [r_hd[k]])
        o0 = ar.alloc([128, S], F32, "o0")
        r_o0 = Res()
        rc = [ar.alloc([128, 512], F32, "rc") for _ in range(2)]
        r_rc = [Res(), Res()]
        ot = [ar.alloc([128, 512], F32, "ot") for _ in range(2)]
        r_ot = [Res(), Res()]
        sq = [ar.alloc([128, 512], BF16, "sq") for _ in range(2)]
        r_sq = [Res(), Res()]
        rs_t = [ar.alloc([128, 512], F32, "rs") for _ in range(2)]
        r_rs = [Res(), Res()]
        ystage = [ar.alloc([128, S], BF16, "ystage") for _ in range(2)]
        r_ys = [Res(), Res()]
        accb = [(self.banks[4], self.banks[5]), (self.banks[6], self.banks[7])]
        r_acc = [Res(), Res()]
        cnt = {"fin": 0}
        gcount = 0
        units = []
        for h in range(6):
            k = h % 2

            def load(h=h, k=k):
                self.dma("sp", qT[k][0][0:64, :], self.qk[h * 128:h * 128 + 64, :], writes=[r_hd[k]])
                self.dma("sp", qT[k][1][64:128, :], self.qk[h * 128 + 64:(h + 1) * 128, :], writes=[r_hd[k]])
                self.dma("sp", kT[k][:, :], self.qk[(6 + h) * 128:(7 + h) * 128, :], writes=[r_hd[k]])
                self.dma("sp", V[k][:, :, :],
                         self.va[:, h * 128:(h + 1) * 128].rearrange("(kb p) e -> p kb e", p=128), writes=[r_hd[k]])
                self.dma("sp", R[k][:, :], self.rscr[h], writes=[r_hd[k]])
            usteps = []
            for c in range(2):
                for qc in range(4):
                    par = gcount % 2
                    gid = gcount
                    gcount += 1
                    numb, denb = accb[par]
                    qs = slice(qc * 512, (qc + 1) * 512)

                    def fin(h=h, c=c, qs=qs, numb=numb, denb=denb, par=par):
                        fi = cnt["fin"]
                        cnt["fin"] += 1
                        f2 = fi % 2
                        rc_, rrc = rc[f2], r_rc[f2]
                        self.defer(lambda: sc.op("act", lambda e: e.activation(rc_[:, :], denb[:, :], AF.Ln),
                                                 [r_acc[par]], [rrc]))
                        self.defer(lambda: sc.op("act", lambda e: e.activation(rc_[:, :], rc_[:, :], AF.Exp, scale=-1.0),
                                                 [rrc], [rrc]))
                        if c == 0:
                            self.defer(lambda: sc.op("dve", lambda e: e.tensor_tensor(
                                o0[:, qs], numb[:, :], rc_[:, :], ALU.mult), [r_acc[par], rrc], [r_o0]))
                            return
                        o, ro = ot[f2], r_ot[f2]
                        self.defer(lambda: sc.op("dve", lambda e: e.tensor_tensor(
                            o[:, :], numb[:, :], rc_[:, :], ALU.mult), [r_acc[par], rrc], [ro]))
                        self.defer(lambda: sc.op("dve", lambda e: e.scalar_tensor_tensor(
                            o[:, :], o[:, :], neglam, o0[:, qs], ALU.mult, ALU.add), [ro, r_sm, r_o0], [ro]))
                        sq_, rsq = sq[f2], r_sq[f2]
                        self.defer(lambda: sc.op("dve", lambda e: e.tensor_tensor(
                            sq_[:, :], o[:, :], o[:, :], ALU.mult), [ro], [rsq]))
                        ssb, r_ssb = denb, r_acc[par]
                        self.defer(lambda: sc.op("pe", lambda e: e.matmul(
                            ssb[:, :], self.ones[:, :], sq_[:, :], start=True, stop=True), [rsq], [r_ssb]))
                        rs_, rrs = rs_t[f2], r_rs[f2]
                        self.defer(lambda: sc.op("act", lambda e: e.activation(
                            rs_[:, :], ssb[:, :], AF.Ln, bias=EPS, scale=1.0 / 128.0), [r_ssb], [rrs]))
                        self.defer(lambda: sc.op("act", lambda e: e.activation(
                            rs_[:, :], rs_[:, :], AF.Exp, scale=-0.5), [rrs], [rrs]))
                        ys, rys = ystage[h % 2], r_ys[h % 2]
                        self.defer(lambda: sc.op("dve", lambda e: e.scalar_tensor_tensor(
                            ys[:, qs], o[:, :], gsub[:, 0:1], rs_[:, :], ALU.mult, ALU.mult),
                            [ro, rrs, r_gs], [rys]))
                        if qs.start == 3 * 512:
                            self.defer(lambda: self.dma("sp", self.yT[h * 128:(h + 1) * 128, :], ys[:, :], [rys], []))

                    for kb in range(16):
                        usteps.append(dict(qT=qT[k][c], kT=kT[k], half=c, R=R[k],
                                           V=(lambda kb, k=k: V[k][:, kb, :]), E=128, qc=qc, kb=kb,
                                           first=(kb == 0), last=(kb == 15), acc=(numb[:, :], denb[:, :]),
                                           r_in=[r_hd[k]], gid=gid,
                                           r_acc=r_acc[par], fin=(fin if kb == 15 else None)))
            units.append((load, usteps))
        self.run_units(units)

    def run_units(self, units):
        steps = []
        for u, (load, usteps) in enumerate(units):
            if u < 2:
                load()
            if u + 2 < len(units):
                usteps[-1]["post"] = units[u + 2][0]
            steps.extend(usteps)
        self.attn_run(steps)

    def phase_c(self, l, s):
        sc, ar = self.sc, self.ar
        self.phase_begin("C")
        qT = [[ar.alloc([128, S], BF16, "qT") for _ in range(2)] for _ in range(2)]
        kT = [ar.alloc([128, S], BF16, "kT") for _ in range(2)]
        V = [[ar.alloc([128, 16, 128], BF16, "V") for _ in range(2)] for _ in range(2)]
        R = [ar.alloc([128, RW], BF16, "R") for _ in range(2)]
        r_ck = [Res(), Res()]
        r_R = [Res(), Res()]
        for k in range(2):
            sc.op("dve", lambda e, k=k: e.memset(qT[k][0][64:128, :], 0.0), [], [r_ck[k]])
            sc.op("dve", lambda e, k=k: e.memset(qT[k][1][0:64, :], 0.0), [], [r_ck[k]])
            for hf in range(2):
                oth = 1 - hf
                sc.op("dve", lambda e, k=k, hf=hf, oth=oth: e.memset(V[k][hf][:, :, oth * 64:(oth + 1) * 64], 0.0),
                      [], [r_ck[k]])
        ndn = [ar.alloc([128, S], F32, "ndn") for _ in range(3)]
        ndd = [ar.alloc([128, S], F32, "ndd") for _ in range(3)]
        r_nd = [Res() for _ in range(3)]
        dsum = ar.alloc([128, S], F32, "dsum")
        r_ds = Res()
        ystage = [ar.alloc([128, S], BF16, "ystage") for _ in range(2)]
        r_ys = [Res(), Res()]
        accb = [(self.banks[4], self.banks[5]), (self.banks[6], self.banks[7])]
        r_acc = [Res(), Res()]
        cnt = {"ys": 0}
        gcount = 0
        nload = 0
        nR = 0
        groups = ((128, 1), (512, 4), (2048, 16))
        units = []
        for ip in range(2):
            for g, (win, r) in enumerate(groups):
                k = nload % 2
                nload += 1
                ch = g * 2 + ip
                for half in range(2):
                    hh = g * 4 + 2 * ip + half
                    kr = nR % 2
                    nR += 1

                    def load(k=k, ch=ch, half=half, hh=hh, kr=kr, g=g, ip=ip):
                        if half == 0:
                            self.dma("sp", qT[k][0][0:64, :], self.qk[(12 + ch) * 128:(12 + ch) * 128 + 64, :],
                                     writes=[r_ck[k]])
                            self.dma("sp", qT[k][1][64:128, :], self.qk[(12 + ch) * 128 + 64:(13 + ch) * 128, :],
                                     writes=[r_ck[k]])
                            self.dma("sp", kT[k][:, :], self.qk[(18 + ch) * 128:(19 + ch) * 128, :], writes=[r_ck[k]])
                            for hf in range(2):
                                h2 = g * 4 + 2 * ip + hf
                                self.dma("sp", V[k][hf][:, :, hf * 64:(hf + 1) * 64],
                                         self.vb[:, h2 * 64:(h2 + 1) * 64].rearrange("(kb p) e -> p kb e", p=128),
                                         writes=[r_ck[k]])
                        self.dma("sp", R[kr][:, :], self.rscr[6 + hh], writes=[r_R[kr]])
                    usteps = []
                    span = (win // (2 * r)) * r
                    hs = slice(half * 64, (half + 1) * 64)
                    for qc in range(4):
                        q0 = qc * 512
                        kb_lo = max(0, (q0 - span) // 128)
                        kb_hi = min(15, (q0 + 511 + span) // 128)
                        kbs = list(range(kb_lo, kb_hi + 1))
                        par = gcount % 2
                        gid = gcount
                        gcount += 1
                        numb, denb = accb[par]
                        qs = slice(q0, q0 + 512)

                        def fin(g=g, half=half, qc=qc, qs=qs, hs=hs, numb=numb, denb=denb, par=par, ip=ip):
                            self.defer(lambda: sc.op("dve", lambda e: e.tensor_copy(ndn[g][hs, qs], numb[hs, :]),
                                                     [r_acc[par]], [r_nd[g]]))
                            self.defer(lambda: sc.op("dve", lambda e: e.tensor_copy(ndd[g][hs, qs], denb[hs, :]),
                                                     [r_acc[par]], [r_nd[g]]))
                            if g == 2 and half == 1 and qc == 3:
                                self.defer(lambda: sc.op("dve", lambda e: e.tensor_tensor(
                                    dsum[:, :], ndd[0][:, :], ndd[1][:, :], ALU.add), [r_nd[0], r_nd[1]], [r_ds]))
                                self.defer(lambda: sc.op("dve", lambda e: e.tensor_tensor(
                                    dsum[:, :], dsum[:, :], ndd[2][:, :], ALU.add), [r_ds, r_nd[2]], [r_ds]))
                                self.defer(lambda: sc.op("act", lambda e: e.activation(dsum[:, :], dsum[:, :], AF.Ln),
                                                         [r_ds], [r_ds]))
                                self.defer(lambda: sc.op("act", lambda e: e.activation(
                                    dsum[:, :], dsum[:, :], AF.Exp, scale=-1.0), [r_ds], [r_ds]))
                                for g2 in range(3):
                                    yi = cnt["ys"] % 2
                                    cnt["ys"] += 1
                                    ys, rys = ystage[yi], r_ys[yi]
                                    self.defer(lambda g2=g2, ys=ys, rys=rys: sc.op("dve", lambda e: e.tensor_tensor(
                                        ys[:, :], ndn[g2][:, :], dsum[:, :], ALU.mult), [r_nd[g2], r_ds], [rys]))
                                    ch2 = 6 + g2 * 2 + ip
                                    self.defer(lambda ch2=ch2, ys=ys, rys=rys: self.dma(
                                        "sp", self.yT[ch2 * 128:(ch2 + 1) * 128, :], ys[:, :], [rys], []))

                        for kb in kbs:
                            usteps.append(dict(qT=qT[k][half], kT=kT[k], half=half, R=R[kr],
                                               V=(lambda kb, k=k, half=half: V[k][half][:, kb, :]),
                                               E=64, qc=qc, kb=kb, first=(kb == kbs[0]), last=(kb == kbs[-1]),
                                               acc=(numb[:, :], denb[:, :]), r_in=[r_ck[k], r_R[kr]],
                                               r_acc=r_acc[par], gid=gid,
                                               fin=(fin if kb == kbs[-1] else None)))
                    units.append((load, usteps))
        self.run_units(units)

    def phase_d(self, l, s):
        sc, ar = self.sc, self.ar
        self.phase_begin("D")
        PADW = 8 + S + 8
        cp = [ar.alloc([128, PADW], F32, "cp") for _ in range(2)]
        r_cp = [Res(), Res()]
        inv = [ar.alloc([128, S], F32, "inv") for _ in range(2)]
        r_inv = [Res(), Res()]
        acc = [ar.alloc([128, S], F32, "acc") for _ in range(2)]
        r_ac = [Res(), Res()]
        dpb = [ar.alloc([128, S], BF16, "dpb") for _ in range(2)]
        r_dp = [Res(), Res()]
        pw = [ar.alloc([128, 128], BF16, "pw") for _ in range(2)]
        r_pw = [Res(), Res()]
        psc = ar.alloc([128, 4], F32, "psc")
        r_psc = Res()
        for g in range(4):
            self.dma("sp", psc[:, g:g + 1], self.pool_scale[l, g * 128:(g + 1) * 128].rearrange("(e o) -> e o", o=1),
                     writes=[r_psc])
        ys = [ar.alloc([128, S], BF16, "ys") for _ in range(2)]
        r_ys = [Res(), Res()]
        bks = self.banks[0:4]
        r_bk = [Res() for _ in range(4)]
        for k in range(2):
            sc.op("dve", lambda e, k=k: e.memset(cp[k][:, 0:8], 0.0), [], [r_cp[k]])
            sc.op("dve", lambda e, k=k: e.memset(cp[k][:, 8 + S:PADW], 0.0), [], [r_cp[k]])
        nb = 0
        for g, win in enumerate((2, 4, 8, 16)):
            k = g % 2
            rad = win // 2
            eng = "dve"
            self.dma("sp", cp[k][:, 8:8 + S], self.cscr[g * 128:(g + 1) * 128, :], writes=[r_cp[k]])
            self.dma("sp", inv[k][:, :], self.c_invcnt[g:g + 1, :].partition_broadcast(128), writes=[r_inv[k]])
            self.dma("pool", pw[k][:, :], self.pool_w[l, g], writes=[r_pw[k]])
            a = acc[k]
            offs = [d_ for d_ in range(-rad, rad + 1)]
            sc.op(eng, lambda e, a=a, k=k, o0=offs[0], o1=offs[1]: e.tensor_tensor(
                a[:, :], cp[k][:, 8 + o0:8 + o0 + S], cp[k][:, 8 + o1:8 + o1 + S], ALU.add), [r_cp[k]], [r_ac[k]])
            for o_ in offs[2:]:
                sc.op(eng, lambda e, a=a, k=k, o_=o_: e.tensor_tensor(
                    a[:, :], a[:, :], cp[k][:, 8 + o_:8 + o_ + S], ALU.add), [r_cp[k], r_ac[k]], [r_ac[k]])
            sc.op(eng, lambda e, a=a, k=k: e.tensor_tensor(a[:, :], a[:, :], inv[k][:, :], ALU.mult),
                  [r_ac[k], r_inv[k]], [r_ac[k]])
            sc.op(eng, lambda e, a=a, k=k: e.tensor_tensor(dpb[k][:, :], a[:, :], cp[k][:, 8:8 + S], ALU.subtract),
                  [r_ac[k], r_cp[k]], [r_dp[k]])
            for tc in range(4):
                bi = nb % 4
                nb += 1
                b = bks[bi]
                sc.op("pe", lambda e, b=b, k=k, tc=tc: e.matmul(b[:, :], pw[k][:, :], dpb[k][:, tc * 512:(tc + 1) * 512],
                                                              start=True, stop=True), [r_pw[k], r_dp[k]], [r_bk[bi]])
                sc.op("act", lambda e, b=b, k=k, tc=tc, g=g: e.activation(
                    ys[k][:, tc * 512:(tc + 1) * 512], b[:, :], AF.Copy, scale=psc[:, g:g + 1]),
                    [r_bk[bi], r_psc], [r_ys[k]])
            self.dma("sp", self.yT[(12 + g) * 128:(13 + g) * 128, :], ys[k][:, :], [r_ys[k]], [])

    def phase_e1(self, l, s):
        sc, ar = self.sc, self.ar
        self.phase_begin("E1")
        yT = ar.alloc([128, 16, S], BF16, "yT")
        r_y = Res()
        for ch in range(16):
            self.dma("sp", yT[:, ch, :], self.yT[ch * 128:(ch + 1) * 128, :], writes=[r_y])
        wp = [ar.alloc([128, 16, 512], BF16, "wp") for _ in range(2)]
        r_wp = [Res(), Res()]
        G = [ar.alloc([128, 3, S], BF16, "G") for _ in range(2)]
        r_G = [Res(), Res()]
        tt_ = [[ar.alloc([128, 512], F32, "tt") for _ in range(3)] for _ in range(2)]
        r_tt = [[Res() for _ in range(3)] for _ in range(2)]
        ms = [ar.alloc([128, S], BF16, "ms") for _ in range(2)]
        r_ms = [Res(), Res()]
        bks = self.banks
        r_bk = [Res() for _ in range(8)]
        nb = 0
        nt = 0
        wa = self.w_proj_a[l].rearrange("(kc p) n -> p kc n", p=128)
        wb_ = self.w_proj_b[l].rearrange("(kc p) n -> p kc n", p=128)
        wc = self.w_proj_c[l].rearrange("(kc p) n -> p kc n", p=128)

        def load_g(dmc):
            for br in range(3):
                self.dma("sp", G[dmc % 2][:, br, :],
                         self.gscr[br * 2048 + dmc * 128:br * 2048 + (dmc + 1) * 128, :], writes=[r_G[dmc % 2]])

        for nbk in range(4):
            w = wp[nbk % 2]
            rw = r_wp[nbk % 2]
            cs = slice(nbk * 512, (nbk + 1) * 512)
            self.dma("pool", w[:, 0:6, :], wa[:, :, cs], writes=[rw])
            self.dma("pool", w[:, 6:12, :], wb_[:, :, cs], writes=[rw])
            self.dma("pool", w[:, 12:16, :], wc[:, :, cs], writes=[rw])
            for dl in range(4):
                dmc = nbk * 4 + dl
                gk = dmc % 2
                if dmc == 0:
                    load_g(0)
                if dmc + 1 < 16:
                    load_g(dmc + 1)
                mst = ms[dmc % 2]
                rms_ = r_ms[dmc % 2]
                for tc in range(4):
                    ts = slice(tc * 512, (tc + 1) * 512)
                    tk = nt % 2
                    nt += 1
                    for br, (k0, k1) in enumerate(((0, 6), (6, 12), (12, 16))):
                        bi = nb % 8
                        nb += 1
                        b = bks[bi]

                        def mm(e, b=b, w=w, dl=dl, ts=ts, k0=k0, k1=k1):
                            ins = None
                            for kc in range(k0, k1):
                                ins = e.matmul(b[:, :], w[:, kc, dl * 128:(dl + 1) * 128], yT[:, kc, ts],
                                               start=(kc == k0), stop=(kc == k1 - 1))
                            return ins
                        sc.op("pe", mm, [rw, r_y], [r_bk[bi]])
                        t = tt_[tk][br]
                        sc.op("dve", lambda e, t=t, b=b, gk=gk, br=br, ts=ts: e.tensor_tensor(
                            t[:, :], b[:, :], G[gk][:, br, ts], ALU.mult), [r_bk[bi], r_G[gk]], [r_tt[tk][br]])
                    t0, t1, t2 = tt_[tk]
                    sc.op("dve", lambda e, t0=t0, t1=t1: e.tensor_tensor(t0[:, :], t0[:, :], t1[:, :], ALU.add),
                          [r_tt[tk][0], r_tt[tk][1]], [r_tt[tk][0]])
                    sc.op("dve", lambda e, t0=t0, t2=t2, mst=mst, ts=ts: e.tensor_tensor(
                        mst[:, ts], t0[:, :], t2[:, :], ALU.add), [r_tt[tk][0], r_tt[tk][2]], [rms_])
                self.dma("sp", self.mT[dmc * 128:(dmc + 1) * 128, :], mst[:, :], [rms_], [])

    def proj_residual(self, aT_src, nkc, wsrc, xin, xout):
        sc, ar = self.sc, self.ar
        aT = ar.alloc([128, nkc, S], BF16, "aT")
        r_a = Res()
        for ch in range(nkc):
            self.dma("sp", aT[:, ch, :], aT_src[ch * 128:(ch + 1) * 128, :], writes=[r_a])
        wo = [ar.alloc([128, nkc, 512], BF16, "wo") for _ in range(2)]
        r_wo = [Res(), Res()]
        xt = [ar.alloc([128, 512], F32, "xt") for _ in range(3)]
        r_xt = [Res() for _ in range(3)]
        xo = [ar.alloc([128, 512], F32, "xo") for _ in range(3)]
        r_xo = [Res() for _ in range(3)]
        bks = self.banks
        r_bk = [Res() for _ in range(8)]
        wv = wsrc.rearrange("(kc p) n -> p kc n", p=128)
        n = 0

        def load_x(m):
            if m >= 64:
                return
            nb_, t_ = m // 16, m % 16
            self.dma("sp", xt[m % 3][:, :], xin[t_ * 128:(t_ + 1) * 128, nb_ * 512:(nb_ + 1) * 512],
                     writes=[r_xt[m % 3]])

        load_x(0)
        for nbk in range(4):
            w = wo[nbk % 2]
            rw = r_wo[nbk % 2]
            cs = slice(nbk * 512, (nbk + 1) * 512)
            self.dma("pool", w[:, :, :], wv[:, :, cs], writes=[rw])
            for tt in range(16):
                rs_ = slice(tt * 128, (tt + 1) * 128)
                i3 = n % 3
                bi = n % 8
                load_x(n + 1)
                n += 1
                b = bks[bi]

                def mm(e, b=b, w=w, rs_=rs_):
                    ins = None
                    for kc in range(nkc):
                        ins = e.matmul(b[:, :], aT[:, kc, rs_], w[:, kc, :], start=(kc == 0), stop=(kc == nkc - 1))
                    return ins
                sc.op("pe", mm, [r_a, rw], [r_bk[bi]])
                sc.op("dve", lambda e, b=b, i3=i3: e.tensor_tensor(xo[i3][:, :], b[:, :], xt[i3][:, :], ALU.add),
                      [r_bk[bi], r_xt[i3]], [r_xo[i3]])
                self.dma("sp", xout[rs_, cs], xo[i3][:, :], [r_xo[i3]], [])

    def phase_e2(self, l, s, xin, xout):
        self.phase_begin("E2")
        self.proj_residual(self.mT, 16, self.w_out[l], xin, xout)

    def phase_f1(self, l, s, xsrc):
        sc, ar = self.sc, self.ar
        self.phase_begin("F1")
        hT = ar.alloc([128, 16, S], BF16, "hT")
        r_hT = Res()
        self.norm_to_hT(xsrc, self.norm2_g[l:l + 1, :], hT, r_hT)
        wb = [ar.alloc([128, 16, 512], BF16, "wb") for _ in range(2)]
        r_wb = [Res(), Res()]
        rl = [ar.alloc([128, 512], F32, "rl") for _ in range(3)]
        r_rl = [Res() for _ in range(3)]
        ust = [ar.alloc([128, S], BF16, "ust") for _ in range(3)]
        r_us = [Res() for _ in range(3)]
        bks = self.banks[2:8]
        r_bk = [Res() for _ in range(6)]
        wv = self.w_up[l].rearrange("(kc p) n -> p kc n", p=128)
        n = 0
        for jb in range(DFF // 512):
            w = wb[jb % 2]
            rw = r_wb[jb % 2]
            self.dma("pool", w[:, :, :], wv[:, :, jb * 512:(jb + 1) * 512], writes=[rw])
            for lc in range(4):
                ffc = jb * 4 + lc
                us = ust[ffc % 3]
                rus = r_us[ffc % 3]
                for tc in range(4):
                    bi = n % 6
                    ri = n % 3
                    n += 1
                    b = bks[bi]

                    def mm(e, b=b, w=w, lc=lc, tc=tc):
                        ins = None
                        for kc in range(16):
                            ins = e.matmul(b[:, :], w[:, kc, lc * 128:(lc + 1) * 128], hT[:, kc, tc * 512:(tc + 1) * 512],
                                           start=(kc == 0), stop=(kc == 15))
                        return ins
                    sc.op("pe", mm, [r_hT, rw], [r_bk[bi]])
                    r_ = rl[ri]
                    sc.op("act", lambda e, r_=r_, b=b: e.activation(r_[:, :], b[:, :], AF.Relu), [r_bk[bi]], [r_rl[ri]])
                    sc.op("dve", lambda e, r_=r_, us=us, tc=tc: e.tensor_tensor(
                        us[:, tc * 512:(tc + 1) * 512], r_[:, :], r_[:, :], ALU.mult), [r_rl[ri]], [rus])
                dst = bass.AP(self.us.tensor, ffc * 128, [[64 * 128, 128], [128 * 64 * 128, 16], [1, 128]])
                self.dma("sp", dst, us[:, :].rearrange("p (t q) -> p t q", t=16), [rus], [])

    def phase_f2(self, l, s, xin, xout):
        sc, ar = self.sc, self.ar
        self.phase_begin("F2")
        NQ = 4
        wd = [ar.alloc([128, 16, 512], BF16, "wd") for _ in range(8)]
        r_wd = [Res() for _ in range(8)]
        ut = [ar.alloc([128, 64, 128], BF16, "ut") for _ in range(2)]
        r_ut = [Res(), Res()]

        def load_ux(m):
            if m >= 64:
                return
            nb_, t_ = m // 16, m % 16
            self.dma("sp", ut[m % 2][:, :, :], self.us[t_], writes=[r_ut[m % 2]])
            self.dma("sp", xt[m % 3][:, :], xin[t_ * 128:(t_ + 1) * 128, nb_ * 512:(nb_ + 1) * 512],
                     writes=[r_xt[m % 3]])

        xt = [ar.alloc([128, 512], F32, "xt") for _ in range(3)]
        r_xt = [Res() for _ in range(3)]
        xo = [ar.alloc([128, 512], F32, "xo") for _ in range(3)]
        r_xo = [Res() for _ in range(3)]
        bks = self.banks
        r_bk = [Res() for _ in range(8)]
        wv = self.w_down[l].rearrange("(kc p) n -> p kc n", p=128)
        n = 0
        for nbk in range(4):
            cs = slice(nbk * 512, (nbk + 1) * 512)
            wq = []
            for q in range(NQ):
                wi = (nbk * NQ + q) % 8
                self.dma("pool", wd[wi][:, :, :], wv[:, q * 16:(q + 1) * 16, cs], writes=[r_wd[wi]])
                wq.append((wd[wi], r_wd[wi]))
            for tt in range(16):
                rs_ = slice(tt * 128, (tt + 1) * 128)
                i3 = n % 3
                bi = n % 8
                u = ut[n % 2]
                ru = r_ut[n % 2]
                if n == 0:
                    load_ux(0)
                load_ux(n + 1)
                n += 1
                b = bks[bi]

                def mm(e, b=b, wq=wq, u=u):
                    ins = None
                    for kc in range(64):
                        w = wq[kc // 16][0]
                        ins = e.matmul(b[:, :], u[:, kc, :], w[:, kc % 16, :], start=(kc == 0), stop=(kc == 63))
                    return ins
                sc.op("pe", mm, [ru] + [q_[1] for q_ in wq], [r_bk[bi]])
                sc.op("dve", lambda e, b=b, i3=i3: e.tensor_tensor(xo[i3][:, :], b[:, :], xt[i3][:, :], ALU.add),
                      [r_bk[bi], r_xt[i3]], [r_xo[i3]])
                self.dma("sp", xout[rs_, cs], xo[i3][:, :], [r_xo[i3]], [])

    def phase_final(self, s, xsrc, dst):
        sc, ar = self.sc, self.ar
        self.phase_begin("FIN")
        gbc = ar.alloc([128, D], F32, "gbc")
        r_g = Res()
        self.dma("sp", gbc[:, :], self.final_g.partition_broadcast(128), writes=[r_g])
        xt = [ar.alloc([128, D], F32, "xt") for _ in range(3)]
        r_xt = [Res() for _ in range(3)]
        junk = ar.alloc([128, D], BF16, "junk")
        r_junk = Res()
        ss = [ar.alloc([128, 1], F32, "ss") for _ in range(3)]
        r_ss = [Res() for _ in range(3)]
        xo = [ar.alloc([128, D], F32, "xo") for _ in range(3)]
        r_xo = [Res() for _ in range(3)]
        for tt in range(S // 128):
            k = tt % 3
            rs_ = slice(tt * 128, (tt + 1) * 128)
            self.dma("sp", xt[k][:, :], xsrc[rs_, :], writes=[r_xt[k]])
            sc.op("act", lambda e, k=k: e.activation(junk[:, :], xt[k][:, :], AF.Square, scale=1.0 / math.sqrt(D),
                                                     accum_out=ss[k][:, :]), [r_xt[k]], [r_junk, r_ss[k]])
            sc.op("act", lambda e, k=k: e.activation(ss[k][:, :], ss[k][:, :], AF.Sqrt, bias=EPS), [r_ss[k]], [r_ss[k]])
            sc.op("dve", lambda e, k=k: e.reciprocal(ss[k][:, :], ss[k][:, :]), [r_ss[k]], [r_ss[k]])
            sc.op("dve", lambda e, k=k: e.scalar_tensor_tensor(xo[k][:, :], xt[k][:, :], ss[k][:, 0:1], gbc[:, :],
                                                               ALU.mult, ALU.mult), [r_xt[k], r_ss[k], r_g], [r_xo[k]])
            self.dma("sp", dst[rs_, :], xo[k][:, :], [r_xo[k]], [])

    def build(self):
        nc, sc = self.nc, self.sc
        self.setup()
        done = False
        for l in range(self.n_layers):
            for s in range(self.n_seq):
                sl = slice(s * S, (s + 1) * S)
                xprev = self.x if l == 0 else self.xb
                stages = [("a", lambda: self.phase_a(l, s, xprev[sl, :])),
                          ("b", lambda: self.phase_b(l, s)),
                          ("c", lambda: self.phase_c(l, s)),
                          ("d", lambda: self.phase_d(l, s)),
                          ("e1", lambda: self.phase_e1(l, s)),
                          ("e2", lambda: self.phase_e2(l, s, xprev[sl, :], self.xa[sl, :])),
                          ("f1", lambda: self.phase_f1(l, s, self.xa[sl, :])),
                          ("f2", lambda: self.phase_f2(l, s, self.xa[sl, :], self.xb[sl, :]))]
                for name, fn in stages:
                    fn()
                    if self.stop_after == name:
                        done = True
                        break
                if done:
                    break
            if done:
                break
        if not done:
            for s in range(self.n_seq):
                sl = slice(s * S, (s + 1) * S)
                self.phase_final(s, self.xb[sl, :], self.out[sl, :])
        fin = sc.finish()
        with ExitStack() as st:
            engsem = {e: st.enter_context(nc.semaphore(f"es_{e}")) for e in ENG}
            dmasems = {e: [st.enter_context(nc.semaphore(f"ds_{e}_{i}")) for i in range(NDMASEM)]
                       for e in ("sp", "pool")}
            block = st.enter_context(nc.Block())
            sc.emit(nc, block, engsem, dmasems)
        return nc


def t5_bucket_np(rel):
    n = np.abs(rel)
    sign_off = np.where(rel > 0, 16, 0)
    thr = np.array([15, 27, 50, 91, 166, 305, 559])
    large = 8 + (n[..., None] >= thr).sum(-1)
    return sign_off + np.where(n < 8, n, large)


def make_consts():
    j = np.arange(GW)
    rel = U0 + 127 - j
    bucket = t5_bucket_np(rel)
    onehot = np.zeros((32, GW), np.float32)
    onehot[bucket, j] = 1.0
    mask = np.zeros((18, GW), np.float32)
    mask[0:6] = 1.0
    for g, (win, r) in enumerate(((128, 1), (512, 4), (2048, 16))):
        ok = ((rel % r) == 0) & (np.abs(rel) <= (win // (2 * r)) * r)
        mask[6 + 4 * g:10 + 4 * g] = ok.astype(np.float32)[None, :]
    mask[:, GW - 1] = 0.0
    ident = np.eye(128, dtype=np.float32)
    antiid = np.ascontiguousarray(ident[::-1])
    pos = np.arange(S)
    invcnt = np.zeros((4, S), np.float32)
    for g, win in enumerate((2, 4, 8, 16)):
        rad = win // 2
        lo = np.clip(pos - rad, 0, S)
        hi = np.clip(pos + rad + 1, 0, S)
        invcnt[g] = 1.0 / (hi - lo).astype(np.float32)
    return {"c_onehot": onehot, "c_mask": mask, "c_ident": ident, "c_antiid": antiid, "c_invcnt": invcnt}


_PROG_CACHE = {}


def kernel(**inputs):
    n = 8
    x = np.ascontiguousarray(np.asarray(inputs["x"], dtype=np.float32))
    B = x.shape[0]
    per = B // n
    consts = make_consts()
    shared = {}
    for k, v in inputs.items():
        if k == "x":
            continue
        a = np.ascontiguousarray(np.asarray(v, dtype=np.float32))
        if k == "final_g":
            a = a.reshape(1, D)
        shared[k] = a
    shared.update(consts)
    if "prog" not in _PROG_CACHE:
        _PROG_CACHE["prog"] = Prog().build()
    nc = _PROG_CACHE["prog"]
    in_maps = []
    for c in range(n):
        m = dict(shared)
        m["x"] = x[c * per:(c + 1) * per].reshape(per * S, D)
        in_maps.append(m)
    res = run_bass_kernel_spmd(nc, in_maps, core_ids=list(range(n)))
    outs = [r["out"].reshape(per, S, D) for r in res.results]
    return np.concatenate(outs, axis=0).astype(np.float32)
```
